# Optimizing a Trainium2 kernel written in Bass

```python
import jax, jax.numpy as jnp
from jax import lax
import numpy as np

D_MODEL = 2048
BATCH = 2
SEQ = 4096
DEPTH = 1
DEC_BATCH = 32
DEC_SEQ = 32
PAST_LEN = 4096

CHUNK = 64
Q_BLOCK = 128
H_A = 8
DK_A = 128
DV_A = 128
W_A = H_A * DK_A
WV_A = H_A * DV_A
H_B = 8
DH_B = 128
W_B = H_B * DH_B
D_FF = 4 * D_MODEL
N_BRANCH = 2
EPS = 1e-6
FOX_FORGET_BIAS = 2.0
IN_SIZES = (W_A, W_A, WV_A, WV_A, W_B, W_B, W_B, H_B, N_BRANCH * D_MODEL)
N_IN = W_A * 2 + WV_A * 2 + W_B * 3 + H_B + N_BRANCH * D_MODEL

kernel_name = 'hgrn2_fox_gated_parallel_streaming_step'


def _in_splits():
    return [int(v) for v in np.cumsum(IN_SIZES)[:-1]]


def rmsnorm(x, g):
    xf = x.astype(jnp.float32)
    y = xf * lax.rsqrt(jnp.mean(xf * xf, axis=-1, keepdims=True) + EPS)
    return (y * g.astype(jnp.float32)).astype(x.dtype)


def _gla_block(S, blk):
    q, k, v, lf = blk
    C = q.shape[1]
    b = jnp.cumsum(lf, axis=1)
    causal = jnp.tril(jnp.ones((C, C), dtype=bool))
    diff = b[:, :, None] - b[:, None, :]
    decay = jnp.exp(jnp.where(causal[None, :, :, None, None], diff, -jnp.inf))
    attn = jnp.einsum('bthd,bshd,btshd->bths', q, k, decay)
    o = (jnp.einsum('bthd,bhde->bthe', q * jnp.exp(b), S)
         + jnp.einsum('bths,bshe->bthe', attn, v))
    b_end = b[:, -1]
    S_new = (jnp.exp(b_end)[..., None] * S
             + jnp.einsum('bshd,bshe->bhde', k * jnp.exp(b_end[:, None] - b), v))
    return S_new, o


def hgrn2_mix(S0, q, k, v, lf):
    B, L = q.shape[0], q.shape[1]
    if L <= CHUNK:
        return _gla_block(S0, (q, k, v, lf))
    n = L // CHUNK
    def blocks(a):
        return jnp.moveaxis(a.reshape((B, n, CHUNK) + a.shape[2:]), 1, 0)
    S_new, o = lax.scan(_gla_block, S0, (blocks(q), blocks(k), blocks(v), blocks(lf)))
    o = jnp.moveaxis(o, 0, 1).reshape((B, L) + o.shape[3:])
    return S_new, o


def fox_attend(q, cq, qpos, k, v, ck, kpos):
    s = jnp.einsum('bthd,bshd->bhts', q, k).astype(jnp.float32) * (DH_B ** -0.5)
    s = s + jnp.moveaxis(cq, 2, 1)[..., None] - jnp.moveaxis(ck, 2, 1)[:, :, None, :]
    mask = kpos[None, :] <= qpos[:, None]
    p = jax.nn.softmax(jnp.where(mask[None, None], s, -jnp.inf), axis=-1)
    return jnp.einsum('bhts,bshd->bthd', p.astype(v.dtype), v)


def fox_prompt(q, k, v, lf):
    B, L = q.shape[0], q.shape[1]
    c = jnp.cumsum(lf, axis=1)
    pos = jnp.arange(L)
    nb = L // Q_BLOCK
    qb = jnp.moveaxis(q.reshape(B, nb, Q_BLOCK, H_B, DH_B), 1, 0)
    cb = jnp.moveaxis(c.reshape(B, nb, Q_BLOCK, H_B), 1, 0)
    pb = pos.reshape(nb, Q_BLOCK)
    o = lax.map(lambda a: fox_attend(a[0], a[1], a[2], k, v, c, pos), (qb, cb, pb))
    return jnp.moveaxis(o, 0, 1).reshape(B, L, H_B, DH_B)


def fox_sample(q, k, v, lf, ck, cv, clf):
    P = ck.shape[1]
    T = q.shape[1]
    k_all = jnp.concatenate([ck, k], axis=1)
    v_all = jnp.concatenate([cv, v], axis=1)
    c_all = jnp.cumsum(jnp.concatenate([clf.astype(jnp.float32), lf], axis=1), axis=1)
    kpos = jnp.arange(P + T)
    return fox_attend(q, c_all[:, P:], P + jnp.arange(T), k_all, v_all, c_all, kpos)


def _layer(x, S0, past, norm1, w_in, b_fox_f, lb, gnorm_a, w_pa, w_pb, w_o, norm2, w1, w2):
    B, L, _ = x.shape
    h = rmsnorm(x, norm1)
    proj = h @ w_in
    a_q, a_f, a_i, a_g, b_q, b_k, b_v, b_fl, gate = jnp.split(proj, _in_splits(), axis=-1)
    fa = lb + (1.0 - lb) * jax.nn.sigmoid(a_f.astype(jnp.float32))
    lf_a = jnp.log(fa).reshape(B, L, H_A, DK_A)
    qa = jax.nn.silu(a_q).reshape(B, L, H_A, DK_A)
    ka = (1.0 - fa).reshape(B, L, H_A, DK_A)
    va = a_i.reshape(B, L, H_A, DV_A)
    S_new, oa = hgrn2_mix(S0, qa, ka, va, lf_a)
    oa = rmsnorm(oa, gnorm_a) * jax.nn.silu(a_g.astype(jnp.float32)).reshape(B, L, H_A, DV_A)
    oa = oa.reshape(B, L, WV_A).astype(x.dtype)
    qb = b_q.reshape(B, L, H_B, DH_B)
    kb = b_k.reshape(B, L, H_B, DH_B)
    vb = b_v.reshape(B, L, H_B, DH_B)
    lf_b = jax.nn.log_sigmoid(b_fl.astype(jnp.float32) + b_fox_f.astype(jnp.float32))
    if past is None:
        ob = fox_prompt(qb, kb, vb, lf_b)
    else:
        ob = fox_sample(qb, kb, vb, lf_b, past[0], past[1], past[2])
    ob = ob.reshape(B, L, W_B).astype(x.dtype)
    g = jax.nn.sigmoid(gate.astype(jnp.float32)).astype(x.dtype)
    g_a, g_b = jnp.split(g, 2, axis=-1)
    merged = g_a * (oa @ w_pa) + g_b * (ob @ w_pb)
    x = x + merged @ w_o
    u = jnp.square(jax.nn.relu(rmsnorm(x, norm2) @ w1))
    x = x + u @ w2
    return x, S_new, kb, vb, lf_b


def setup_inputs(seed: int = 0) -> dict:
    key = jax.random.key(seed)
    ks = jax.random.split(key, 20)
    f32 = jnp.float32
    def nrm(k, shape, scale):
        return jax.random.normal(k, shape, f32) * scale
    return {
        'x_prompt': nrm(ks[0], (BATCH, SEQ, D_MODEL), 1.0),
        'x_sample': nrm(ks[1], (DEC_BATCH, DEC_SEQ, D_MODEL), 1.0),
        'cache_fox_k': nrm(ks[2], (DEPTH, DEC_BATCH, PAST_LEN, H_B, DH_B), 1.0),
        'cache_fox_v': nrm(ks[3], (DEPTH, DEC_BATCH, PAST_LEN, H_B, DH_B), 1.0),
        'cache_fox_logf': jax.nn.log_sigmoid(nrm(ks[4], (DEPTH, DEC_BATCH, PAST_LEN, H_B), 1.0) + FOX_FORGET_BIAS),
        'state_hgrn': nrm(ks[5], (DEPTH, DEC_BATCH, H_A, DK_A, DV_A), 0.5),
        'norm1': 1.0 + nrm(ks[6], (DEPTH, D_MODEL), 0.01),
        'w_in': nrm(ks[7], (DEPTH, D_MODEL, N_IN), D_MODEL ** -0.5),
        'b_fox_f': FOX_FORGET_BIAS + nrm(ks[8], (DEPTH, H_B), 0.1),
        'lb_logits': nrm(ks[9], (DEPTH + 1, W_A), 0.1),
        'gnorm_a': 1.0 + nrm(ks[10], (DEPTH, DV_A), 0.01),
        'w_pa': nrm(ks[11], (DEPTH, WV_A, D_MODEL), WV_A ** -0.5),
        'w_pb': nrm(ks[12], (DEPTH, W_B, D_MODEL), W_B ** -0.5),
        'w_o': nrm(ks[13], (DEPTH, D_MODEL, D_MODEL), D_MODEL ** -0.5),
        'norm2': 1.0 + nrm(ks[14], (DEPTH, D_MODEL), 0.01),
        'w1': nrm(ks[15], (DEPTH, D_MODEL, D_FF), D_MODEL ** -0.5),
        'w2': nrm(ks[16], (DEPTH, D_FF, D_MODEL), D_FF ** -0.5),
        'norm_f': 1.0 + nrm(ks[17], (D_MODEL,), 0.01),
    }


def reference(x_prompt, x_sample, cache_fox_k, cache_fox_v, cache_fox_logf, state_hgrn,
              norm1, w_in, b_fox_f, lb_logits, gnorm_a, w_pa, w_pb, w_o, norm2, w1, w2, norm_f):
    lower = jnp.cumsum(jax.nn.softmax(lb_logits.astype(jnp.float32), axis=0), axis=0)
    xp, xs = x_prompt, x_sample
    kp_l, vp_l, lfp_l, Sp_l, ks_l, vs_l, lfs_l, Ss_l = [], [], [], [], [], [], [], []
    for l in range(DEPTH):
        wts = (norm1[l], w_in[l], b_fox_f[l], lower[l], gnorm_a[l], w_pa[l], w_pb[l],
               w_o[l], norm2[l], w1[l], w2[l])
        S0 = jnp.zeros((xp.shape[0], H_A, DK_A, DV_A), jnp.float32)
        xp, Sp, kp, vp, lfp = _layer(xp, S0, None, *wts)
        xs, Ss, ks, vs, lfs = _layer(xs, state_hgrn[l],
                                     (cache_fox_k[l], cache_fox_v[l], cache_fox_logf[l]), *wts)
        kp_l.append(kp); vp_l.append(vp); lfp_l.append(lfp); Sp_l.append(Sp)
        ks_l.append(ks); vs_l.append(vs); lfs_l.append(lfs); Ss_l.append(Ss)
    y_prompt = rmsnorm(xp, norm_f)
    y_sample = rmsnorm(xs, norm_f)
    return (y_prompt, y_sample,
            jnp.stack(kp_l), jnp.stack(vp_l), jnp.stack(lfp_l), jnp.stack(Sp_l),
            jnp.stack(ks_l), jnp.stack(vs_l), jnp.stack(lfs_l), jnp.stack(Ss_l))
```

```python
import contextlib
import numpy as np
import concourse.bass as bass
import concourse.mybir as mybir
from concourse.bass_utils import run_bass_kernel_spmd

F32 = mybir.dt.float32
BF16 = mybir.dt.bfloat16
AF = mybir.ActivationFunctionType
ALU = mybir.AluOpType

ENGS = ("pe", "act", "dve", "pool", "sp")
EPS = 1e-6
BIG = 30000.0


class _Op:
    __slots__ = ("eng", "fn", "reads", "writes", "dma", "deps", "signal", "tick", "sem")

    def __init__(self, eng, fn, reads, writes, dma):
        self.eng = eng
        self.fn = fn
        self.reads = reads
        self.writes = writes
        self.dma = dma
        self.deps = set()
        self.signal = False
        self.tick = 0
        self.sem = None


class Sched:
    def __init__(self, nc):
        self.nc = nc
        self.ops = []
        self.last_w = {}
        self.readers = {}
        self.fence_idx = None
        self.seen = set()

    def op(self, eng, fn, reads=(), writes=(), dma=None):
        o = _Op(eng, fn, tuple(reads), tuple(writes), dma)
        idx = len(self.ops)
        deps = o.deps
        for k in o.reads + o.writes:
            if k not in self.seen:
                self.seen.add(k)
                if self.fence_idx is not None:
                    deps.add(self.fence_idx)
        ops = self.ops

        def add(p, war):
            q = ops[p]
            if q.dma is None and dma is None and q.eng == eng:
                if eng == "pe" or war:
                    return
            deps.add(p)
        for k in o.reads:
            w = self.last_w.get(k)
            if w is not None:
                add(w, False)
            if isinstance(k, tuple) and k[0] in ("PB", "PTB"):
                for r in self.readers.get(k, ()):
                    if ops[r].eng != eng:
                        deps.add(r)
        for k in o.writes:
            w = self.last_w.get(k)
            if w is not None:
                add(w, False)
            for r in self.readers.get(k, ()):
                add(r, True)
        for k in o.reads:
            self.readers.setdefault(k, []).append(idx)
        for k in o.writes:
            self.last_w[k] = idx
            self.readers[k] = []
        deps.discard(idx)
        self.ops.append(o)
        return idx

    def fence(self, fn):
        o = _Op("pool", fn, (), (), None)
        idx = len(self.ops)
        for k, w in self.last_w.items():
            o.deps.add(w)
        for k, rs in self.readers.items():
            for r in rs:
                o.deps.add(r)
        if self.fence_idx is not None:
            o.deps.add(self.fence_idx)
        self.ops.append(o)
        self.fence_idx = idx
        allk = set(self.last_w) | set(self.readers)
        self.last_w = {k: idx for k in allk}
        self.readers = {k: [] for k in allk}
        return idx

    def emit(self, stack, final_dma_slots=()):
        nc = self.nc
        ops = self.ops
        for o in ops:
            for d in o.deps:
                ops[d].signal = True
        eng_sem = {e: stack.enter_context(nc.semaphore("s_" + e)) for e in ENGS}
        dma_sem, dma_cnt = {}, {}
        eng_cnt = {e: 0 for e in ENGS}
        for o in ops:
            if o.dma is not None:
                if o.dma not in dma_sem:
                    dma_sem[o.dma] = stack.enter_context(nc.semaphore("d_" + str(o.dma)))
                    dma_cnt[o.dma] = 0
                dma_cnt[o.dma] += 16
                o.tick = dma_cnt[o.dma]
                o.sem = dma_sem[o.dma]
                o.signal = True
            elif o.signal:
                eng_cnt[o.eng] += 1
                o.tick = eng_cnt[o.eng]
                o.sem = eng_sem[o.eng]
        for i in range(len(ops) - 2, -1, -1):
            o, nx = ops[i], ops[i + 1]
            if o.dma is not None and nx.dma == o.dma:
                o.tick = nx.tick
        self.n_sems = len(dma_sem) + len(ENGS)
        final = [(dma_sem[s], dma_cnt[s]) for s in final_dma_slots if s in dma_sem]

        def run_engine(ename, eng):
            known = {}
            for o in ops:
                if o.eng != ename:
                    continue
                need = {}
                for d in o.deps:
                    p = ops[d]
                    sid = id(p.sem)
                    if need.get(sid, (None, 0))[1] < p.tick:
                        need[sid] = (p.sem, p.tick)
                for sid, (sem, tick) in need.items():
                    if known.get(sid, 0) >= tick:
                        continue
                    eng.wait_ge(sem, tick)
                    known[sid] = tick
                ins = o.fn(eng)
                if o.signal:
                    assert ins is not None
                    ins.then_inc(o.sem, 16 if o.dma is not None else 1)
            if ename == "sp":
                for sem, cnt in final:
                    eng.wait_ge(sem, cnt)

        block = stack.enter_context(nc.Block())

        @block.tensor
        def _(eng):
            run_engine("pe", eng)

        @block.scalar
        def _(eng):
            run_engine("act", eng)

        @block.vector
        def _(eng):
            run_engine("dve", eng)

        @block.gpsimd
        def _(eng):
            run_engine("pool", eng)

        @block.sync
        def _(eng):
            run_engine("sp", eng)


def _consts():
    s = np.arange(128)[:, None]
    t = np.arange(128)[None, :]
    c = {}
    c["ID"] = (s == t)
    c["U2"] = (s <= t) & (s // 64 == t // 64)
    c["SU2"] = (s > t) & (s // 64 == t // 64)
    c["U4"] = (s <= t) & (s // 32 == t // 32)
    c["SU4"] = (s > t) & (s // 32 == t // 32)
    c["UF"] = (s <= t)
    c["SUF"] = (s > t)
    c["SEL127"] = (s == 127) & (t >= 0)
    c["ALL1"] = (s >= 0) & (t >= 0)
    for k in range(2):
        c["CM2_%d" % k] = (t // 64 == k) & (s >= 0)
    for k in range(4):
        c["CM4_%d" % k] = (t // 32 == k) & (s >= 0)
    for k in range(4):
        c["SR4_%d" % k] = (s // 32 == k) & (t >= 0)
    offs = {}
    cols = []
    o = 0
    for k, v in c.items():
        offs[k] = o
        cols.append(v.astype(np.float32))
        o += 128
    rm2 = np.zeros((128, 2), np.float32)
    rm2[np.arange(128), np.arange(128) // 64] = 1
    rm4 = np.zeros((128, 4), np.float32)
    rm4[np.arange(128), np.arange(128) // 32] = 1
    offs["RM2"] = o
    cols.append(rm2)
    o += 2
    offs["RM4"] = o
    cols.append(rm4)
    o += 4
    return np.ascontiguousarray(np.concatenate(cols, axis=1)), offs


_CST, _COFF = _consts()
NCST = _CST.shape[1]

AQ, AFo, AI, AG, BQ, BK, BV, BFL, GA = 0, 1024, 2048, 3072, 4096, 5120, 6144, 7168, 7176


def build(cfg):
    D = cfg["D"]
    TB = cfg["TB"]
    PT = cfg["PT"]
    KC = D // 128
    NW = 4 * TB
    NTO = TB + 1
    TOK = NTO * 128
    DFF = 4 * D
    NIN = GA + 2 * D
    PW = min(512, D)
    NPW = D // PW
    PWC = PW // 128
    FW = min(256, DFF)
    NFP = DFF // FW
    FWC = FW // 128
    SC = 128.0 ** -0.5

    nc = bass.Bass("TRN2", target_bir_lowering=False)

    def din(name, shape):
        return nc.dram_tensor(name, list(shape), F32, kind="ExternalInput").ap()

    def dout(name, shape):
        return nc.dram_tensor(name, list(shape), F32, kind="ExternalOutput").ap()

    xw = din("xw", [NW * 128, D])
    xs = din("xs", [128, D])
    kmask_d = din("kmask", [128, NW])
    ck_d = din("ck", [4, PT * 128, 1024])
    cv_d = din("cv", [4, PT * 128, 1024])
    clf_d = din("clf", [4, PT * 128, 8])
    st0_d = din("st0", [4, 8, 128, 128])
    norm1_d = din("norm1", [1, D])
    w_in_d = din("w_in", [D, NIN])
    bff_d = din("b_fox_f", [1, 8])
    lbl_d = din("lb_logits", [2, 1024])
    gn_d = din("gnorm_a", [1, 128])
    w_pa_d = din("w_pa", [1024, D])
    w_pb_d = din("w_pb", [1024, D])
    w_o_d = din("w_o", [D, D])
    norm2_d = din("norm2", [1, D])
    w1_d = din("w1", [D, DFF])
    w2_d = din("w2", [DFF, D])
    normf_d = din("norm_f", [1, D])
    cst_d = din("consts", [128, NCST])

    y_d = dout("y", [TOK, D])
    ko_d = dout("ko", [TOK, 1024])
    vo_d = dout("vo", [TOK, 1024])
    lfo_d = dout("lfo", [TOK, 8])
    So_d = dout("So", [8, 128, 128])
    Ss_d = dout("Ss", [4, 8, 128, 128])

    KTs = nc.dram_tensor("KTs", [8, 128, (NW + 1) * 128], BF16, kind="Internal").ap()
    Vs = nc.dram_tensor("Vs", [(NW + 1) * 128, 1024], BF16, kind="Internal").ap()
    Gs = nc.dram_tensor("Gs", [TOK, 2 * D], F32, kind="Internal").ap()

    S = Sched(nc)
    outer = contextlib.ExitStack()
    with outer:
        def SB(st, name, shape, dt):
            return st.enter_context(nc.sbuf_tensor(name, list(shape), dt))

        def PS(name, shape, dt):
            return outer.enter_context(nc.psum_tensor(name, list(shape), dt))

        PB = [PS("pb%d" % i, [128, 512], F32) for i in range(6)]
        PTB = [PS("ptb%d" % i, [128, 1024], BF16) for i in range(2)]
        cnt = {"g": 0, "t": 0}

        cst = SB(outer, "cst", [128, NCST], F32)
        idb = SB(outer, "idb", [128, 128], BF16)
        OML = SB(outer, "OML", [128, 1024], F32)
        GN = SB(outer, "GN", [128, 128], F32)
        BFF = SB(outer, "BFF", [128, 8], F32)
        gvec = SB(outer, "gvec", [128, D], F32)
        S32 = SB(outer, "S32", [128, 8, 128], F32)
        LFB = SB(outer, "LFB", [128, NW + 1, 8], F32)
        CKM = SB(outer, "CKM", [128, NW, 8], F32)
        CAR = SB(outer, "CAR", [128, NW + 1, 8], F32)
        KMK = SB(outer, "KMK", [128, NW], F32)
        wfl = SB(outer, "wfl", [128, KC, 8], BF16)
        HT = SB(outer, "HT", [128, KC, TOK], BF16)

        def C(name, w=128):
            o = _COFF[name]
            return cst[:, o:o + w]

        S.op("sp", lambda e: e.dma_start(out=cst[:], in_=cst_d), writes=["cst"], dma="cst")
        S.op("dve", lambda e: e.tensor_copy(out=idb[:], in_=C("ID")), reads=["cst"], writes=["idb"])
        S.op("sp", lambda e: e.dma_start(out=GN[:], in_=gn_d.broadcast_to([128, 128])), writes=["GN"], dma="GN")
        S.op("sp", lambda e: e.dma_start(out=BFF[:], in_=bff_d.broadcast_to([128, 8])), writes=["BFF"], dma="BFF")
        S.op("sp", lambda e: e.dma_start(out=KMK[:], in_=kmask_d), writes=["KMK"], dma="KMK")
        S.op("pool", lambda e: e.dma_start(out=wfl[:], in_=w_in_d[:, BFL:BFL + 8].rearrange("(c p) n -> p c n", p=128)),
             writes=["wfl"], dma="wfl")
        with contextlib.ExitStack() as st0:
            L0 = SB(st0, "L0", [128, 1024], F32)
            L1 = SB(st0, "L1", [128, 1024], F32)
            S.op("sp", lambda e: e.dma_start(out=L0[:], in_=lbl_d[0:1, :].broadcast_to([128, 1024])), writes=["L0"], dma="L0")
            S.op("sp", lambda e: e.dma_start(out=L1[:], in_=lbl_d[1:2, :].broadcast_to([128, 1024])), writes=["L1"], dma="L1")
            S.op("dve", lambda e: e.tensor_tensor(out=L0[:], in0=L0[:], in1=L1[:], op=ALU.subtract), reads=["L0", "L1"], writes=["L0"])
            S.op("act", lambda e: e.activation(out=L0[:], in_=L0[:], func=AF.Exp), reads=["L0"], writes=["L0"])
            S.op("dve", lambda e: e.tensor_scalar(out=L0[:], in0=L0[:], scalar1=1.0, scalar2=None, op0=ALU.add), reads=["L0"], writes=["L0"])
            S.op("dve", lambda e: e.reciprocal(out=OML[:], in_=L0[:]), reads=["L0"], writes=["OML"])
        S.op("pool", lambda e: e.memset(S32[:], 0.0), writes=["S32"])
        S.op("pool", lambda e: e.memset(CAR[:, 0, :], 0.0), writes=[("CAR", 0)])

        def gbank():
            i = cnt["g"] % 2
            cnt["g"] += 1
            return PB[i], ("PB", i)

        def tbank():
            i = cnt["t"] % 2
            cnt["t"] += 1
            return PTB[i], ("PTB", i)

        def rmsnorm_rstd(src_ap, src_keys, junk, ss, rstd, tag, n, junk_key=None):
            S.op("act", lambda e: e.activation(out=junk, in_=src_ap, func=AF.Square, accum_out=ss),
                 reads=src_keys, writes=[junk_key if junk_key is not None else tag + "junk", tag + "ss"])
            S.op("act", lambda e: e.activation(out=rstd, in_=ss, func=AF.Ln, scale=1.0 / n, bias=EPS),
                 reads=[tag + "ss"], writes=[tag + "rstd"])
            S.op("act", lambda e: e.activation(out=rstd, in_=rstd, func=AF.Exp, scale=-0.5),
                 reads=[tag + "rstd"], writes=[tag + "rstd"])

        def transposes_to(src_bf, src_keys, nchunk, dst_fn, dst_keys, evac_eng):
            c0 = 0
            while c0 < nchunk:
                n = min(8, nchunk - c0)
                tb, tk = tbank()

                def tr(e, c0=c0, n=n, tb=tb):
                    ins = None
                    for c in range(n):
                        ins = e.transpose(out=tb[:, c * 128:(c + 1) * 128], in_=src_bf[:, (c0 + c) * 128:(c0 + c + 1) * 128],
                                          identity=idb[:])
                    return ins
                S.op("pe", tr, reads=list(src_keys) + ["idb"], writes=[tk])
                dst = dst_fn(c0, n)
                src = tb[:, 0:n * 128].rearrange("p (c t) -> p c t", c=n)
                if evac_eng == "act":
                    S.op("act", lambda e, dst=dst, src=src: e.activation(out=dst, in_=src, func=AF.Copy), reads=[tk], writes=dst_keys)
                else:
                    S.op("dve", lambda e, dst=dst, src=src: e.tensor_copy(out=dst, in_=src), reads=[tk], writes=dst_keys)
                c0 += n

        def gemm_tile(lt, act_T, act_keys, kchunks, wp_ap, wp_keys, ncols, bank, bkey):
            def mm(e):
                ins = None
                for c in range(kchunks):
                    ins = e.matmul(bank[:, 0:ncols], lhsT=act_T[:, c, lt * 128:(lt + 1) * 128], rhs=wp_ap[:, c, 0:ncols],
                                   start=(c == 0), stop=(c == kchunks - 1))
                return ins
            S.op("pe", mm, reads=list(act_keys) + list(wp_keys), writes=[bkey])

        mid = contextlib.ExitStack()
        QT = SB(mid, "QT", [128, 8, TOK], BF16)
        oaT = SB(mid, "oaT", [128, 8, TOK], BF16)

        p1 = contextlib.ExitStack()
        wp = [SB(p1, "wp%d" % i, [128, KC, 512], BF16) for i in range(2)]
        xt = [SB(p1, "xt%d" % i, [128, D], F32) for i in range(2)]
        hb = [SB(p1, "hb%d" % i, [128, D], BF16) for i in range(1)] * 2
        nss = [SB(p1, "nss%d" % i, [128, 1], F32) for i in range(2)]
        nrs = [SB(p1, "nrs%d" % i, [128, 1], F32) for i in range(2)]
        NSET = 4
        hg = []
        for i in range(NSET):
            d = {}
            d["E3"] = SB(p1, "E3_%d" % i, [128, 384], F32)
            d["T3"] = SB(p1, "T3_%d" % i, [128, 384], F32)
            d["SL"] = SB(p1, "SL_%d" % i, [128, 384], F32)
            d["u"] = SB(p1, "u_%d" % i, [128, 128], F32)
            d["ka"] = SB(p1, "ka_%d" % i, [128, 128], F32)
            d["ebT"] = SB(p1, "ebT_%d" % i, [128, 4], F32)
            d["vb"] = SB(p1, "vb_%d" % i, [128, 128], BF16)
            d["qt"] = SB(p1, "qt_%d" % i, [128, 256], BF16)
            d["kp"] = SB(p1, "kp_%d" % i, [128, 128], BF16)
            d["qkT"] = SB(p1, "qkT_%d" % i, [128, 2, 128], BF16)
            d["atb"] = SB(p1, "atb_%d" % i, [128, 128], BF16)
            d["gs"] = SB(p1, "gs_%d" % i, [128, 128], F32)
            d["oss"] = SB(p1, "oss_%d" % i, [128, 1], F32)
            d["ors"] = SB(p1, "ors_%d" % i, [128, 1], F32)
            d["oa"] = SB(p1, "oa_%d" % i, [128, 128], BF16)
            d["eb"] = d["E3"][:, 0:128]
            d["enb"] = d["E3"][:, 128:256]
            d["ekp"] = d["E3"][:, 256:384]
            d["lf"] = d["u"]
            d["oj"] = d["T3"][:, 0:128]
            hg.append(d)
        smp_qTc = SB(p1, "smp_qTc", [128, 4, 128], BF16)
        smp_vc = SB(p1, "smp_vc", [128, 4, 128], BF16)
        for d in hg:
            d["qTc"] = smp_qTc
            d["vc"] = smp_vc
        Sring = [SB(p1, "Sring%d" % i, [128, 128], BF16) for i in range(8)]
        SS32 = [SB(p1, "SS32_%d" % i, [128, 4, 128], F32) for i in range(1)] * 2
        SSbf = [SB(p1, "SSbf_%d" % i, [128, 4, 128], BF16) for i in range(1)] * 2
        SSo = [SB(p1, "SSo_%d" % i, [128, 4, 128], F32) for i in range(1)] * 2
        kst = [SB(p1, "kst%d" % i, [128, 512], F32) for i in range(2)]
        kbf = [SB(p1, "kbf%d" % i, [128, 512], BF16) for i in range(2)]
        KTst = [SB(p1, "KTst%d" % i, [128, 4, 128], BF16) for i in range(2)]
        fz = [SB(p1, "fz%d" % i, [128, 8], F32) for i in range(2)]
        gE = kst

        PBM, PBAT, PBSU, PBO = PB[2], PB[3], PB[4], PB[5]
        hcnt = {"n": 0}

        def hgrn_item(blk, lt, h, mode, bank, bkey, tokcol):
            si = hcnt["n"] % NSET
            hcnt["n"] += 1
            d = hg[si]
            _al = {"eb": "E3", "enb": "E3", "ekp": "E3", "lf": "u", "oj": "T3"}
            K = lambda n: ("hg", si, _al.get(n, n)) if n not in ("qTc", "vc") else ("smp", n)
            own = mode != "pre"
            nch = 4 if mode == "smp" else 2
            Uc, SUc = ("U4", "SU4") if mode == "smp" else ("U2", "SU2")
            RMo = _COFF["RM4"] if mode == "smp" else _COFF["RM2"]
            CMn = "CM4_%d" if mode == "smp" else "CM2_%d"
            if own:
                fo, io, qo, go = 128, 384, 0, 256
                ne = 384
            else:
                fo, io = 0, 128
                ne = 128
            E3, T3, SL = d["E3"], d["T3"], d["SL"]
            S.op("act", lambda e: e.activation(out=E3[:, 0:ne], in_=bank[:, 0:ne], func=AF.Exp, scale=-1.0), reads=[bkey], writes=[K("E3")])
            S.op("act", lambda e: e.activation(out=d["vb"][:], in_=bank[:, io:io + 128], func=AF.Copy), reads=[bkey], writes=[K("vb")])
            S.op("act", lambda e: e.activation(out=T3[:, 0:ne], in_=E3[:, 0:ne], func=AF.Ln, bias=1.0), reads=[K("E3")], writes=[K("T3")])
            S.op("act", lambda e: e.activation(out=T3[:, 0:ne], in_=T3[:, 0:ne], func=AF.Exp, scale=-1.0), reads=[K("T3")], writes=[K("T3")])
            if own:
                S.op("dve", lambda e: e.tensor_tensor(out=SL[:], in0=bank[:, 0:384], in1=T3[:], op=ALU.mult),
                     reads=[bkey, K("T3")], writes=[K("SL")])
            S.op("dve", lambda e: e.tensor_tensor(out=d["u"][:], in0=E3[:, fo:fo + 128], in1=T3[:, fo:fo + 128], op=ALU.mult),
                 reads=[K("E3"), K("T3")], writes=[K("u")])
            S.op("pool", lambda e: e.tensor_tensor(out=d["ka"][:], in0=d["u"][:], in1=OML[:, h * 128:(h + 1) * 128], op=ALU.mult),
                 reads=[K("u"), "OML"], writes=[K("ka")])
            S.op("act", lambda e: e.activation(out=d["lf"][:], in_=d["ka"][:], func=AF.Ln, scale=-1.0, bias=1.0),
                 reads=[K("ka")], writes=[K("lf")])
            def cum(e):
                ins = None
                if own:
                    ins = e.matmul(PBM[:, 0:128], lhsT=C(Uc), rhs=d["lf"][:], start=True, stop=True)
                ins = e.matmul(PBM[:, 128:256], lhsT=C(SUc), rhs=d["lf"][:], start=True, stop=True)
                ins = e.matmul(PBM[:, 256:256 + nch], lhsT=d["lf"][:], rhs=cst[:, RMo:RMo + nch], start=True, stop=True)
                return ins
            S.op("pe", cum, reads=[K("lf"), "cst"], writes=[("PB", 2)])
            if own:
                S.op("act", lambda e: e.activation(out=d["eb"][:], in_=PBM[:, 0:128], func=AF.Exp), reads=[("PB", 2)], writes=[K("eb")])
                S.op("act", lambda e: e.activation(out=d["enb"][:], in_=PBM[:, 0:128], func=AF.Exp, scale=-1.0), reads=[("PB", 2)], writes=[K("enb")])
            S.op("act", lambda e: e.activation(out=d["ekp"][:], in_=PBM[:, 128:256], func=AF.Exp), reads=[("PB", 2)], writes=[K("ekp")])
            S.op("act", lambda e: e.activation(out=d["ebT"][:, 0:nch], in_=PBM[:, 256:256 + nch], func=AF.Exp), reads=[("PB", 2)], writes=[K("ebT")])
            S.op("dve", lambda e: e.tensor_tensor(out=d["kp"][:], in0=d["ka"][:], in1=d["ekp"][:], op=ALU.mult),
                 reads=[K("ka"), K("ekp")], writes=[K("kp")])
            for c in range(nch):
                S.op("pool", lambda e, c=c: e.tensor_scalar(out=d["vc"][:, c, :], in0=d["vb"][:], scalar1=cst[:, RMo + c:RMo + c + 1],
                                                            scalar2=1.0, op0=ALU.mult, op1=ALU.mult),
                     reads=[K("vb"), "cst"], writes=[K("vc", ) + (c,)])
            if own:
                S.op("dve", lambda e: e.tensor_tensor(out=d["qt"][:, 0:128], in0=SL[:, 0:128], in1=d["eb"][:], op=ALU.mult),
                     reads=[K("SL"), K("eb")], writes=[K("qt0")])
                S.op("dve", lambda e: e.tensor_tensor(out=d["qt"][:, 128:256], in0=d["ka"][:], in1=d["enb"][:], op=ALU.mult),
                     reads=[K("ka"), K("enb")], writes=[K("qt1")])
                S.op("pool", lambda e: e.tensor_tensor(out=d["gs"][:], in0=SL[:, 256:384], in1=GN[:], op=ALU.mult),
                     reads=[K("SL"), "GN"], writes=[K("gs")])
                transposes_to(d["qt"], [K("qt0"), K("qt1")], 2, lambda c0, n: d["qkT"][:, c0:c0 + n, :], [K("qkT")], "act")
                S.op("pe", lambda e: e.matmul(PBAT[:, 0:128], lhsT=d["qkT"][:, 1, :], rhs=d["qkT"][:, 0, :], start=True, stop=True),
                     reads=[K("qkT")], writes=[("PB", 3)])
                S.op("dve", lambda e: e.tensor_tensor(out=d["atb"][:], in0=PBAT[:, 0:128], in1=C(Uc), op=ALU.mult),
                     reads=[("PB", 3), "cst"], writes=[K("atb")])
                for c in range(nch):
                    S.op("pool", lambda e, c=c: e.tensor_tensor(out=d["qTc"][:, c, :], in0=d["qkT"][:, 0, :], in1=C(CMn % c), op=ALU.mult),
                         reads=[K("qkT"), "cst"], writes=[K("qTc") + (c,)])
            if mode == "smp":
                sslot = 0
                S.op("sp", lambda e: e.dma_start(out=SS32[sslot][:], in_=st0_d[:, h, :, :].rearrange("r d e -> d r e")),
                     writes=[("SS32", sslot)], dma=("SS32", sslot))
                S.op("act", lambda e: e.activation(out=SSbf[sslot][:], in_=SS32[sslot][:], func=AF.Copy),
                     reads=[("SS32", sslot)], writes=[("SSbf", sslot)])
            for c in range(nch):
                if own:
                    if mode == "smp":
                        rhs_state, skey = SSbf[0][:, c, :], ("SSbf", 0)
                    else:
                        rhs_state, skey = Sbf[c % 2][:, h, :], ("Sbf", c % 2, h)
                    S.op("pe", lambda e, c=c, rhs_state=rhs_state: e.matmul(PBO[:, 0:128], lhsT=d["qTc"][:, c, :], rhs=rhs_state,
                                                                            start=(c == 0), stop=False),
                         reads=[K("qTc") + (c,), skey], writes=[("PB", 5)])
                S.op("pe", lambda e, c=c: e.matmul(PBSU[:, 0:128], lhsT=d["kp"][:], rhs=d["vc"][:, c, :], start=True, stop=True),
                     reads=[K("kp"), K("vc") + (c,)], writes=[("PB", 4)])
                if mode == "smp":
                    sslot = 0
                    S.op("dve", lambda e, c=c, sslot=sslot: e.scalar_tensor_tensor(
                        out=SSo[sslot][:, c, :], in0=SS32[sslot][:, c, :], scalar=d["ebT"][:, c:c + 1], in1=PBSU[:, 0:128],
                        op0=ALU.mult, op1=ALU.add), reads=[("SS32", sslot), K("ebT"), ("PB", 4)], writes=[("SSo", sslot, c)])
                else:
                    S.op("dve", lambda e, c=c: e.scalar_tensor_tensor(
                        out=S32[:, h, :], in0=S32[:, h, :], scalar=d["ebT"][:, c:c + 1], in1=PBSU[:, 0:128],
                        op0=ALU.mult, op1=ALU.add), reads=[("S32", h), K("ebT"), ("PB", 4)], writes=[("S32", h)])
                    if blk == 3 or (blk == 2 and lt == TB - 1 and c == nch - 1):
                        nslot = (c + 1) % 2
                        S.op("act", lambda e, nslot=nslot: e.activation(out=Sbf[nslot][:, h, :], in_=S32[:, h, :], func=AF.Copy),
                             reads=[("S32", h)], writes=[("Sbf", nslot, h)])
            if mode == "smp":
                sslot = 0
                S.op("sp", lambda e: e.dma_start(out=Ss_d[:, h, :, :].rearrange("r d e -> d r e"), in_=SSo[sslot][:]),
                     reads=[("SSo", sslot, c) for c in range(4)], dma=("SSoD", sslot))
            if own:
                S.op("pe", lambda e: e.matmul(PBO[:, 0:128], lhsT=d["atb"][:], rhs=d["vb"][:], start=False, stop=True),
                     reads=[K("atb"), K("vb")], writes=[("PB", 5)])
                rmsnorm_rstd(PBO[:, 0:128], [("PB", 5)], d["oj"], d["oss"][:], d["ors"][:], "hg%d" % si, 128, junk_key=K("T3"))
                S.op("dve", lambda e: e.scalar_tensor_tensor(out=d["oa"][:], in0=PBO[:, 0:128], scalar=d["ors"][:], in1=d["gs"][:],
                                                             op0=ALU.mult, op1=ALU.mult),
                     reads=[("PB", 5), "hg%drstd" % si, K("gs")], writes=[K("oa")])
                transposes_to(d["oa"], [K("oa")], 1, lambda c0, n: oaT[:, h:h + 1, tokcol:tokcol + 128], [("oaT", h, lt)], "dve")

        gcnt = {"n": 0, "m": 0, "su": 0}

        def hgrn_gen(blk, lt, h, mode, wp_ap, wkeys, ncols, ring0):
            si = gcnt["n"] % NSET
            gcnt["n"] += 1
            d = hg[si]
            _al = {"eb": "E3", "enb": "E3", "ekp": "E3", "lf": "u", "oj": "T3"}
            K = lambda n: ("hg", si, _al.get(n, n))
            own = mode == "own"
            E3, T3, SL = d["E3"], d["T3"], d["SL"]
            bank, bkey = gbank()
            gemm_tile(lt, HT, [("HT", lt)], KC, wp_ap, wkeys, ncols, bank, bkey)
            yield
            mi = 2 + gcnt["m"] % 2
            gcnt["m"] += 1
            PM, pmk = PB[mi], ("PB", mi)
            if own:
                fo, io = 128, 384
                S.op("act", lambda e: e.activation(out=E3[:], in_=bank[:, 0:384], func=AF.Exp, scale=-1.0), reads=[bkey], writes=[K("E3")])
                S.op("act", lambda e: e.activation(out=d["vb"][:], in_=bank[:, io:io + 128], func=AF.Copy), reads=[bkey], writes=[K("vb")])
                S.op("act", lambda e: e.activation(out=T3[:], in_=E3[:], func=AF.Ln, bias=1.0), reads=[K("E3")], writes=[K("T3")])
                S.op("act", lambda e: e.activation(out=T3[:], in_=T3[:], func=AF.Exp, scale=-1.0), reads=[K("T3")], writes=[K("T3")])
                yield
                S.op("dve", lambda e: e.tensor_tensor(out=d["u"][:], in0=E3[:, fo:fo + 128], in1=T3[:, fo:fo + 128], op=ALU.mult),
                     reads=[K("E3"), K("T3")], writes=[K("u")])
                S.op("dve", lambda e: e.tensor_tensor(out=d["ka"][:], in0=d["u"][:], in1=OML[:, h * 128:(h + 1) * 128], op=ALU.mult),
                     reads=[K("u"), "OML"], writes=[K("ka")])
                S.op("dve", lambda e: e.tensor_tensor(out=SL[:], in0=bank[:, 0:384], in1=T3[:], op=ALU.mult),
                     reads=[bkey, K("T3")], writes=[K("SL")])
                S.op("pool", lambda e: e.tensor_tensor(out=d["gs"][:], in0=SL[:, 256:384], in1=GN[:], op=ALU.mult),
                     reads=[K("SL"), "GN"], writes=[K("gs")])
            else:
                fo, io = 0, 128
                S.op("act", lambda e: e.activation(out=E3[:, 0:128], in_=bank[:, 0:128], func=AF.Exp), reads=[bkey], writes=[K("E3")])
                S.op("act", lambda e: e.activation(out=d["vb"][:], in_=bank[:, io:io + 128], func=AF.Copy), reads=[bkey], writes=[K("vb")])
                yield
                S.op("dve", lambda e: e.tensor_scalar(out=T3[:, 0:128], in0=E3[:, 0:128], scalar1=1.0, scalar2=None, op0=ALU.add),
                     reads=[K("E3")], writes=[K("T3")])
                S.op("dve", lambda e: e.reciprocal(out=T3[:, 0:128], in_=T3[:, 0:128]), reads=[K("T3")], writes=[K("T3")])
                S.op("dve", lambda e: e.tensor_tensor(out=d["ka"][:], in0=T3[:, 0:128], in1=OML[:, h * 128:(h + 1) * 128], op=ALU.mult),
                     reads=[K("T3"), "OML"], writes=[K("ka")])
            yield
            S.op("act", lambda e: e.activation(out=d["lf"][:], in_=d["ka"][:], func=AF.Ln, scale=-1.0, bias=1.0), reads=[K("ka")], writes=[K("lf")])
            yield
            RMo = _COFF["RM2"]

            def cum(e):
                ins = None
                if own:
                    ins = e.matmul(PM[:, 0:128], lhsT=C("U2"), rhs=d["lf"][:], start=True, stop=True)
                ins = e.matmul(PM[:, 128:256], lhsT=C("SU2"), rhs=d["lf"][:], start=True, stop=True)
                ins = e.matmul(PM[:, 256:258], lhsT=d["lf"][:], rhs=cst[:, RMo:RMo + 2], start=True, stop=True)
                return ins
            S.op("pe", cum, reads=[K("lf"), "cst"], writes=[pmk])
            yield
            if own:
                S.op("act", lambda e: e.activation(out=d["eb"], in_=PM[:, 0:128], func=AF.Exp), reads=[pmk], writes=[K("eb")])
                S.op("act", lambda e: e.activation(out=d["enb"], in_=PM[:, 0:128], func=AF.Exp, scale=-1.0), reads=[pmk], writes=[K("enb")])
            S.op("act", lambda e: e.activation(out=d["ekp"], in_=PM[:, 128:256], func=AF.Exp), reads=[pmk], writes=[K("ekp")])
            S.op("act", lambda e: e.activation(out=d["ebT"][:, 0:2], in_=PM[:, 256:258], func=AF.Exp), reads=[pmk], writes=[K("ebT")])
            yield
            S.op("dve", lambda e: e.tensor_tensor(out=d["kp"][:], in0=d["ka"][:], in1=d["ekp"], op=ALU.mult),
                 reads=[K("ka"), K("ekp")], writes=[K("kp")])
            if own:
                S.op("dve", lambda e: e.tensor_tensor(out=d["qt"][:, 0:128], in0=SL[:, 0:128], in1=d["eb"], op=ALU.mult),
                     reads=[K("SL"), K("eb")], writes=[K("qt0")])
                S.op("dve", lambda e: e.tensor_tensor(out=d["qt"][:, 128:256], in0=d["ka"][:], in1=d["enb"], op=ALU.mult),
                     reads=[K("ka"), K("enb")], writes=[K("qt1")])
                yield
                transposes_to(d["qt"], [K("qt0"), K("qt1")], 2, lambda c0, n: d["qkT"][:, c0:c0 + n, :], [K("qkT")], "act")
            yield
            if own:
                vc2 = d["u"][:].bitcast(BF16).rearrange("p (c e) -> p c e", c=2)
                RMo2 = _COFF["RM2"]
                for c in range(2):
                    S.op("pool", lambda e, c=c: e.tensor_scalar(out=vc2[:, c, :], in0=d["vb"][:], scalar1=cst[:, RMo2 + c:RMo2 + c + 1],
                                                                scalar2=1.0, op0=ALU.mult, op1=ALU.mult),
                         reads=[K("vb"), "cst"], writes=[K("u")])
                PSU, psk = PB[4], ("PB", 4)
                PSU1, psk1 = PB[4], ("PB", 4)
                su1_ap = PSU[:, 256:384]

                def su(e):
                    e.matmul(PSU[:, 0:128], lhsT=d["qkT"][:, 1, :], rhs=d["qkT"][:, 0, :], start=True, stop=True)
                    e.matmul(PSU[:, 128:256], lhsT=d["kp"][:], rhs=vc2[:, 0, :], start=True, stop=True)
                    return e.matmul(PSU[:, 256:384], lhsT=d["kp"][:], rhs=vc2[:, 1, :], start=True, stop=True)
                S.op("pe", su, reads=[K("kp"), K("u"), K("qkT")], writes=[psk])
            else:
                PSU, psk = PB[4], ("PB", 4)
                PSU1, psk1 = PB[5], ("PB", 5)
                su1_ap = PSU1[:, 0:128]

                def su(e):
                    e.matmul(PSU[:, 128:256], lhsT=d["kp"][0:64, :], rhs=d["vb"][0:64, :], start=True, stop=True)
                    return e.matmul(PSU1[:, 0:128], lhsT=d["kp"][64:128, :], rhs=d["vb"][64:128, :], start=True, stop=True)
                S.op("pe", su, reads=[K("kp"), K("vb")], writes=[psk, psk1])
            yield
            if own:
                S.op("dve", lambda e: e.tensor_tensor(out=d["atb"][:], in0=PSU[:, 0:128], in1=C("U2"), op=ALU.mult),
                     reads=[psk, "cst"], writes=[K("atb")])
            if own and lt == 0:
                S.op("pool", lambda e: e.tensor_copy(out=Sring[ring0 % 8][:], in_=S32[:, h, :]),
                     reads=[("S32", h)], writes=[("Sring", ring0 % 8)])
            for c in range(2):
                src_ap, src_k = (PSU[:, 128:256], psk) if c == 0 else (su1_ap, psk1)
                S.op("dve", lambda e, c=c, src_ap=src_ap: e.scalar_tensor_tensor(
                    out=S32[:, h, :], in0=S32[:, h, :], scalar=d["ebT"][:, c:c + 1], in1=src_ap,
                    op0=ALU.mult, op1=ALU.add), reads=[("S32", h), K("ebT"), src_k], writes=[("S32", h)])
                if own and not (lt == TB - 1 and c == 1):
                    rs = (ring0 + c + 1) % 8
                    S.op("pool", lambda e, rs=rs: e.tensor_copy(out=Sring[rs][:], in_=S32[:, h, :]),
                         reads=[("S32", h)], writes=[("Sring", rs)])
            if not own:
                return
            yield
            ra, rb = ring0 % 8, (ring0 + 1) % 8

            def omm(e):
                e.matmul(PB[5][0:64, 0:128], lhsT=d["qkT"][:, 0, 0:64], rhs=Sring[ra][:], start=True, stop=False)
                e.matmul(PB[5][0:64, 0:128], lhsT=d["atb"][:, 0:64], rhs=d["vb"][:], start=False, stop=True)
                e.matmul(PB[5][64:128, 0:128], lhsT=d["qkT"][:, 0, 64:128], rhs=Sring[rb][:], start=True, stop=False)
                return e.matmul(PB[5][64:128, 0:128], lhsT=d["atb"][:, 64:128], rhs=d["vb"][:], start=False, stop=True)
            S.op("pe", omm, reads=[K("qkT"), K("atb"), K("vb"), ("Sring", ra), ("Sring", rb)], writes=[("PB", 5)])
            yield
            o32 = SL[:, 0:128]
            S.op("act", lambda e: e.activation(out=o32, in_=PB[5][:, 0:128], func=AF.Copy), reads=[("PB", 5)], writes=[K("SL")])
            rmsnorm_rstd(PB[5][:, 0:128], [("PB", 5)], d["oj"], d["oss"][:], d["ors"][:], "hg%d" % si, 128, junk_key=K("T3"))
            S.op("dve", lambda e: e.scalar_tensor_tensor(out=d["oa"][:], in0=o32, scalar=d["ors"][:], in1=d["gs"][:],
                                                         op0=ALU.mult, op1=ALU.mult),
                 reads=[K("SL"), "hg%drstd" % si, K("gs")], writes=[K("oa")])
            yield
            transposes_to(d["oa"], [K("oa")], 1, lambda c0, n: oaT[:, h:h + 1, lt * 128:(lt + 1) * 128], [("oaT", h, lt)], "dve")

        def hgrn_pre2_gen(lt, h0, wp_ap, wkeys):
            si = gcnt["n"] % NSET
            gcnt["n"] += 1
            d = hg[si]
            K = lambda n: ("hg", si, n)
            E = d["E3"][:, 0:256]
            T = d["T3"][:, 0:256]
            ka = d["SL"][:, 0:256]
            lf = d["E3"][:, 0:256]
            ekp = d["T3"][:, 0:256]
            vb2 = d["qt"]
            kp2 = d["qkT"][:].rearrange("p a b -> p (a b)")
            bank, bkey = gbank()
            gemm_tile(lt, HT, [("HT", lt)], KC, wp_ap, wkeys, 512, bank, bkey)
            yield
            S.op("act", lambda e: e.activation(out=E, in_=bank[:, 0:256], func=AF.Exp), reads=[bkey], writes=[K("E3")])
            S.op("act", lambda e: e.activation(out=vb2[:], in_=bank[:, 256:512], func=AF.Copy), reads=[bkey], writes=[K("qt")])
            S.op("act", lambda e: e.activation(out=T, in_=E, func=AF.Ln, bias=1.0), reads=[K("E3")], writes=[K("T3")])
            S.op("act", lambda e: e.activation(out=T, in_=T, func=AF.Exp, scale=-1.0), reads=[K("T3")], writes=[K("T3")])
            yield
            S.op("dve", lambda e: e.tensor_tensor(out=ka, in0=T, in1=OML[:, h0 * 128:(h0 + 2) * 128], op=ALU.mult),
                 reads=[K("T3"), "OML"], writes=[K("SL")])
            yield
            S.op("act", lambda e: e.activation(out=lf, in_=ka, func=AF.Ln, scale=-1.0, bias=1.0), reads=[K("SL")], writes=[K("E3")])
            yield
            mi = 2 + gcnt["m"] % 2
            gcnt["m"] += 1
            PM, pmk = PB[mi], ("PB", mi)
            ao = _COFF["ALL1"]

            def cum(e):
                ins = e.matmul(PM[:, 0:256], lhsT=C("SUF"), rhs=lf, start=True, stop=True)
                for hh in range(2):
                    ins = e.matmul(PM[:, 256 + 2 * hh:258 + 2 * hh], lhsT=lf[:, hh * 128:(hh + 1) * 128], rhs=cst[:, ao:ao + 2],
                                   start=True, stop=True)
                return ins
            S.op("pe", cum, reads=[K("E3"), "cst"], writes=[pmk])
            yield
            S.op("act", lambda e: e.activation(out=ekp, in_=PM[:, 0:256], func=AF.Exp), reads=[pmk], writes=[K("T3")])
            S.op("act", lambda e: e.activation(out=d["ebT"][:, 0:4], in_=PM[:, 256:260], func=AF.Exp), reads=[pmk], writes=[K("ebT")])
            yield
            S.op("dve", lambda e: e.tensor_tensor(out=kp2, in0=ka, in1=ekp, op=ALU.mult), reads=[K("SL"), K("T3")], writes=[K("qkT")])
            yield
            sui = 4 + gcnt["su"] % 2
            gcnt["su"] += 1
            PSU, psk = PB[sui], ("PB", sui)

            def su(e):
                ins = None
                for hh in range(2):
                    ins = e.matmul(PSU[:, hh * 128:(hh + 1) * 128], lhsT=kp2[:, hh * 128:(hh + 1) * 128], rhs=vb2[:, hh * 128:(hh + 1) * 128],
                                   start=True, stop=True)
                return ins
            S.op("pe", su, reads=[K("qkT"), K("qt")], writes=[psk])
            yield
            for hh in range(2):
                S.op("dve", lambda e, hh=hh: e.scalar_tensor_tensor(
                    out=S32[:, h0 + hh, :], in0=S32[:, h0 + hh, :], scalar=d["ebT"][:, 2 * hh:2 * hh + 1], in1=PSU[:, hh * 128:(hh + 1) * 128],
                    op0=ALU.mult, op1=ALU.add), reads=[("S32", h0 + hh), K("ebT"), psk], writes=[("S32", h0 + hh)])

        class _Pipe:
            def __init__(self, depth):
                self.depth = depth
                self.active = []

            def step(self):
                nxt = []
                for g in self.active:
                    try:
                        next(g)
                        nxt.append(g)
                    except StopIteration:
                        pass
                self.active = nxt

            def feed(self, g):
                while len(self.active) >= self.depth:
                    self.step()
                self.active.append(g)
                self.step()

            def drain(self):
                while self.active:
                    self.step()
        PIPE = _Pipe(NSET)

        def run_pipelined(gens, depth):
            active = []
            it = iter(gens)
            exhausted = False
            while True:
                if not exhausted and len(active) < depth:
                    try:
                        active.append(next(it))
                    except StopIteration:
                        exhausted = True
                if not active:
                    break
                nxt = []
                for g in active:
                    try:
                        next(g)
                        nxt.append(g)
                    except StopIteration:
                        pass
                active = nxt

        jobs = []

        def wtile_of(blk, lt):
            return blk * TB + lt if lt < TB else NW

        def next_ws():
            ws = cnt.setdefault("w", 0) % 2
            cnt["w"] += 1
            return ws

        def norm_job(blk, ntile):
            def comp():
                if blk == 0:
                    S.op("sp", lambda e: e.dma_start(out=gvec[:], in_=norm1_d.broadcast_to([128, D])), writes=["gvec"], dma="gvec")
                for lt in range(ntile):
                    sl = lt % 2
                    if lt < TB:
                        src = xw[(blk * TB + lt) * 128:(blk * TB + lt + 1) * 128, :]
                    else:
                        src = xs
                    S.op("sp", lambda e, sl=sl, src=src: e.dma_start(out=xt[sl][:], in_=src), writes=[("xt", sl)], dma=("xt", sl))
                    rmsnorm_rstd(xt[sl][:], [("xt", sl)], hb[sl][:], nss[sl][:], nrs[sl][:], "n%d" % sl, D, junk_key=("hb", 0))
                    S.op("dve", lambda e, sl=sl: e.scalar_tensor_tensor(out=hb[sl][:], in0=xt[sl][:], scalar=nrs[sl][:], in1=gvec[:],
                                                                         op0=ALU.mult, op1=ALU.mult),
                         reads=[("xt", sl), "n%drstd" % sl, "gvec"], writes=[("hb", 0)])
                    transposes_to(hb[sl], [("hb", 0)], KC, lambda c0, n, lt=lt: HT[:, c0:c0 + n, lt * 128:(lt + 1) * 128],
                                  [("HT", lt)], "act" if lt % 2 == 0 else "dve")
            return (None, comp)

        def hgrn_job(blk, ntile, h):
            own = blk == 3
            ws = next_ws()
            cols = [AQ, AFo, AG, AI] if own else [AFo, AI]

            def load():
                for j, co in enumerate(cols):
                    S.op("pool", lambda e, j=j, co=co: e.dma_start(
                        out=wp[ws][:, :, j * 128:(j + 1) * 128],
                        in_=w_in_d[:, co + h * 128:co + (h + 1) * 128].rearrange("(c p) n -> p c n", p=128)),
                        writes=[("wp", ws, j)], dma=("wp", ws))

            def comp():
                wkeys = [("wp", ws, j) for j in range(len(cols))]
                ncols = 128 * len(cols)
                for lt in range(TB):
                    PIPE.feed(hgrn_gen(blk, lt, h, "own" if own else "pre", wp[ws], wkeys, ncols, 2 * lt))
            return (load, comp, "hgrn")

        def smp_job(h):
            ws = next_ws()
            cols = [AQ, AFo, AG, AI]

            def load():
                for j, co in enumerate(cols):
                    S.op("pool", lambda e, j=j, co=co: e.dma_start(
                        out=wp[ws][:, :, j * 128:(j + 1) * 128],
                        in_=w_in_d[:, co + h * 128:co + (h + 1) * 128].rearrange("(c p) n -> p c n", p=128)),
                        writes=[("wp", ws, j)], dma=("wp", ws))

            def comp():
                wkeys = [("wp", ws, j) for j in range(4)]
                lt = TB
                bank, bkey = gbank()
                gemm_tile(lt, HT, [("HT", lt)], KC, wp[ws], wkeys, 512, bank, bkey)
                hgrn_item(3, lt, h, "smp", bank, bkey, lt * 128)
                if h == 7:
                    S.op("sp", lambda e: e.dma_start(out=So_d.rearrange("h d e -> d h e"), in_=S32[:]),
                         reads=[("S32", hh) for hh in range(8)], dma="SoD")
            return (load, comp)

        def hgrn_pre2_job(blk, h0):
            ws = next_ws()

            def load():
                for j, co in enumerate([AFo, AI]):
                    S.op("pool", lambda e, j=j, co=co: e.dma_start(
                        out=wp[ws][:, :, j * 256:(j + 1) * 256],
                        in_=w_in_d[:, co + h0 * 128:co + (h0 + 2) * 128].rearrange("(c p) n -> p c n", p=128)),
                        writes=[("wp", ws, 2 * j), ("wp", ws, 2 * j + 1)], dma=("wp", ws))

            def comp():
                wkeys = [("wp", ws, j) for j in range(4)]
                for lt in range(TB):
                    PIPE.feed(hgrn_pre2_gen(lt, h0, wp[ws], wkeys))
            return (load, comp, "hgrn")

        def kvq_job(blk, ntile, kind, base, p):
            own = blk == 3
            ws = next_ws()

            def load():
                S.op("pool", lambda e: e.dma_start(
                    out=wp[ws][:], in_=w_in_d[:, base + p * 512:base + (p + 1) * 512].rearrange("(c p) n -> p c n", p=128)),
                    writes=[("wp", ws, j) for j in range(4)], dma=("wp", ws))

            def comp():
                wkeys = [("wp", ws, j) for j in range(4)]
                pendk = []
                for lt in range(ntile):
                    bank, bkey = gbank()
                    gemm_tile(lt, HT, [("HT", lt)], KC, wp[ws], wkeys, 512, bank, bkey)
                    while pendk:
                        pendk.pop(0)()
                    sl = cnt.setdefault("kv", 0) % 2
                    cnt["kv"] += 1
                    wt = wtile_of(blk, lt)
                    if own and kind in ("k", "v"):
                        dst = ko_d if kind == "k" else vo_d
                        S.op("act", lambda e, sl=sl, bank=bank: e.activation(out=kst[sl][:], in_=bank[:], func=AF.Copy),
                             reads=[bkey], writes=[("kst", sl)])
                        S.op("sp", lambda e, sl=sl, dst=dst, lt=lt: e.dma_start(
                            out=dst[lt * 128:(lt + 1) * 128, p * 512:(p + 1) * 512], in_=kst[sl][:]),
                            reads=[("kst", sl)], dma=("kstD", sl))
                    S.op("dve", lambda e, sl=sl, bank=bank: e.tensor_copy(out=kbf[sl][:], in_=bank[:]), reads=[bkey], writes=[("kbf", sl)])
                    if kind == "v":
                        S.op("sp", lambda e, sl=sl, wt=wt: e.dma_start(
                            out=Vs[wt * 128:(wt + 1) * 128, p * 512:(p + 1) * 512], in_=kbf[sl][:]),
                            reads=[("kbf", sl)], writes=[("Vs", wt, p)], dma=("kbfD", sl))
                    elif kind == "k":
                        def trk_(sl=sl, wt=wt):
                            transposes_to(kbf[sl], [("kbf", sl)], 4, lambda c0, n: KTst[sl][:, c0:c0 + n, :], [("KTst", sl)], "act")
                            S.op("sp", lambda e: e.dma_start(
                                out=KTs[4 * p:4 * p + 4, :, wt * 128:(wt + 1) * 128].rearrange("h d t -> d h t"), in_=KTst[sl][:]),
                                reads=[("KTst", sl)], writes=[("KTs", wt, p)], dma=("KTstD", sl))
                        pendk.append(trk_)
                    else:
                        def trq_(sl=sl, lt=lt):
                            transposes_to(kbf[sl], [("kbf", sl)], 4,
                                          lambda c0, n: QT[:, 4 * p + c0:4 * p + c0 + n, lt * 128:(lt + 1) * 128],
                                          [("QT", lt, p)], "act")
                        pendk.append(trq_)
                while pendk:
                    pendk.pop(0)()
            return (load, comp)

        def fl_job(blk, ntile):
            own = blk == 3

            def comp():
                for lt in range(ntile):
                    bank, bkey = gbank()
                    gemm_tile(lt, HT, [("HT", lt)], KC, wfl, ["wfl"], 8, bank, bkey)
                    sl = lt % 2
                    wt = wtile_of(blk, lt)
                    S.op("dve", lambda e, sl=sl, bank=bank: e.tensor_tensor(out=fz[sl][:], in0=bank[:, 0:8], in1=BFF[:], op=ALU.add),
                         reads=[bkey, "BFF"], writes=[("fz", sl)])
                    S.op("act", lambda e, sl=sl: e.activation(out=fz[sl][:], in_=fz[sl][:], func=AF.Exp, scale=-1.0), reads=[("fz", sl)], writes=[("fz", sl)])
                    S.op("act", lambda e, sl=sl: e.activation(out=fz[sl][:], in_=fz[sl][:], func=AF.Ln, bias=1.0), reads=[("fz", sl)], writes=[("fz", sl)])
                    S.op("dve", lambda e, sl=sl, wt=wt: e.tensor_scalar(out=LFB[:, wt, :], in0=fz[sl][:], scalar1=-1.0, scalar2=None, op0=ALU.mult),
                         reads=[("fz", sl)], writes=[("LFB", wt)])
                    if own:
                        S.op("sp", lambda e, wt=wt, lt=lt: e.dma_start(out=lfo_d[lt * 128:(lt + 1) * 128, :], in_=LFB[:, wt, :]),
                             reads=[("LFB", wt)], dma="lfoD")
            return (None, comp)

        def gate_job(ntile, gp):
            ws = next_ws()

            def load():
                S.op("pool", lambda e: e.dma_start(
                    out=wp[ws][:, :, 0:PW], in_=w_in_d[:, GA + gp * PW:GA + (gp + 1) * PW].rearrange("(c p) n -> p c n", p=128)),
                    writes=[("wp", ws, j) for j in range(4)], dma=("wp", ws))

            def comp():
                wkeys = [("wp", ws, j) for j in range(4)]
                for lt in range(ntile):
                    bank, bkey = gbank()
                    gemm_tile(lt, HT, [("HT", lt)], KC, wp[ws], wkeys, PW, bank, bkey)
                    sl = cnt["kv"] % 2
                    cnt["kv"] += 1
                    S.op("act", lambda e, sl=sl, bank=bank: e.activation(out=gE[sl][:, 0:PW], in_=bank[:, 0:PW], func=AF.Exp, scale=-1.0),
                         reads=[bkey], writes=[("kst", sl)])
                    S.op("act", lambda e, sl=sl: e.activation(out=gE[sl][:, 0:PW], in_=gE[sl][:, 0:PW], func=AF.Ln, bias=1.0),
                         reads=[("kst", sl)], writes=[("kst", sl)])
                    S.op("act", lambda e, sl=sl: e.activation(out=gE[sl][:, 0:PW], in_=gE[sl][:, 0:PW], func=AF.Exp, scale=-1.0),
                         reads=[("kst", sl)], writes=[("kst", sl)])
                    S.op("sp", lambda e, sl=sl, lt=lt: e.dma_start(out=Gs[lt * 128:(lt + 1) * 128, gp * PW:(gp + 1) * PW], in_=gE[sl][:, 0:PW]),
                         reads=[("kst", sl)], writes=[("Gs", lt, gp)], dma=("kstD", sl))
            return (load, comp)

        for blk in range(4):
            own = blk == 3
            ntile = NTO if own else TB
            jobs.append(norm_job(blk, ntile))
            if own:
                for h in range(8):
                    jobs.append(hgrn_job(blk, ntile, h))
                for h in range(8):
                    jobs.append(smp_job(h))
            else:
                for h0 in range(0, 8, 2):
                    jobs.append(hgrn_pre2_job(blk, h0))
            for kind, base in (("k", BK), ("v", BV)) + ((("q", BQ),) if own else ()):
                for p in range(2):
                    jobs.append(kvq_job(blk, ntile, kind, base, p))
            jobs.append(fl_job(blk, ntile))
            if own:
                for gp in range(2 * NPW):
                    jobs.append(gate_job(ntile, gp))
        if jobs[0][0] is not None:
            jobs[0][0]()
        for k, job in enumerate(jobs):
            if len(job) < 3:
                PIPE.drain()
            if k + 1 < len(jobs) and jobs[k + 1][0] is not None:
                jobs[k + 1][0]()
            job[1]()
        PIPE.drain()
        p1.close()
        S.fence(lambda e: e.memset(CAR[:, 0, :], 0.0))

        mid2 = contextlib.ExitStack()
        obT = SB(mid2, "obT", [128, 8, TOK], BF16)
        p2 = contextlib.ExitStack()
        KTh = [SB(p2, "KTh%d" % i, [128, NW * 128], BF16) for i in range(2)]
        Vh = [SB(p2, "Vh%d" % i, [128, NW, 129], BF16) for i in range(2)]
        biasb = [SB(p2, "biasb%d" % i, [128, NW], F32) for i in range(2)]
        NPT = 8
        pT = [SB(p2, "pT%d" % i, [128, 256], BF16) for i in range(NPT)]
        rden = [SB(p2, "rden%d" % i, [128, 1], F32) for i in range(2)]
        obb = [SB(p2, "obb%d" % i, [128, 128], BF16) for i in range(2)]
        for i in range(2):
            S.op("pool", lambda e, i=i: e.memset(Vh[i][:, :, 128:129], 1.0), writes=[("Vh1", i)])

        PBC = PB[2]
        for m in range(NW + 1):
            if m > 0:
                S.op("pe", lambda e, m=m: e.matmul(PBC[:, 0:8], lhsT=C("SEL127"), rhs=CKM[:, m - 1, :], start=True, stop=True),
                     reads=[("CK", m - 1), "cst"], writes=[("PB", 2)])
                S.op("dve", lambda e, m=m: e.tensor_copy(out=CAR[:, m, :], in_=PBC[:, 0:8]), reads=[("PB", 2)], writes=[("CAR", m)])
            if m < NW:
                S.op("pe", lambda e, m=m: e.matmul(PB[3][:, 16:24], lhsT=C("UF"), rhs=LFB[:, m, :], start=True, stop=True),
                     reads=[("LFB", m), "cst"], writes=[("PB", 3)])
                S.op("dve", lambda e, m=m: e.tensor_tensor(out=CKM[:, m, :], in0=PB[3][:, 16:24], in1=CAR[:, m, :], op=ALU.add),
                     reads=[("PB", 3), ("CAR", m)], writes=[("CK", m)])
        for m in range(NW):
            S.op("dve", lambda e, m=m: e.tensor_scalar(out=CKM[:, m, :], in0=CKM[:, m, :], scalar1=KMK[:, m:m + 1], scalar2=None, op0=ALU.add),
                 reads=[("CK", m), ("CK", min(m + 1, NW - 1)), ("CAR", min(m + 1, NW)), "KMK"], writes=[("CK", m)])

        PBS = [PB[0], PB[1]]
        PBOA = [PB[4], PB[5]]
        acnt = {"s": 0, "p": 0, "o": 0}
        pend = []
        assert TB % 2 == 0

        def flush_pend():
            while pend:
                pend.pop(0)()
        for h in range(8):
            hs = h % 2
            S.op("sp", lambda e, hs=hs, h=h: e.dma_start(out=KTh[hs][:], in_=KTs[h, :, 0:NW * 128]),
                 reads=[("KTs", wt, h // 4) for wt in range(NW)], writes=[("KTh", hs)], dma=("KTh", hs))
            S.op("sp", lambda e, hs=hs, h=h: e.dma_start(out=Vh[hs][:, :, 0:128],
                                                          in_=Vs[0:NW * 128, h * 128:(h + 1) * 128].rearrange("(m p) d -> p m d", p=128)),
                 reads=[("Vs", wt, h // 4) for wt in range(NW)], writes=[("Vh", hs)], dma=("Vh", hs))
            for ip in range(TB // 2):
                i0, i1 = 2 * ip, 2 * ip + 1
                qi0, qi1 = 3 * TB + i0, 3 * TB + i1
                bs = acnt["o"] % 2
                acnt["o"] += 1
                S.op("dve", lambda e, bs=bs, h=h, qi1=qi1: e.tensor_scalar(out=biasb[bs][:], in0=CKM[:, :, h], scalar1=-1.0,
                                                                             scalar2=CAR[:, qi1 + 1, h:h + 1], op0=ALU.mult, op1=ALU.add),
                     reads=[("CK", m) for m in range(NW)] + [("CAR", qi1 + 1)], writes=[("biasb", bs)])
                nkt = qi1 + 1
                m0 = 0
                while m0 < nkt:
                    ng = min(2, nkt - m0)
                    sbi = acnt["s"] % 4
                    acnt["s"] += 1
                    sb_ = PB[sbi]
                    skey = ("PB", sbi)

                    def scm(e, sb_=sb_, hs=hs, m0=m0, ng=ng, h=h, i0=i0, i1=i1, qi1=qi1):
                        ins = None
                        for k in range(ng):
                            m = m0 + k
                            if m == qi1:
                                ins = e.matmul(sb_[:, k * 256 + 128:(k + 1) * 256], lhsT=KTh[hs][:, m * 128:(m + 1) * 128],
                                               rhs=QT[:, h, i1 * 128:(i1 + 1) * 128], start=True, stop=True)
                            else:
                                ins = e.matmul(sb_[:, k * 256:(k + 1) * 256], lhsT=KTh[hs][:, m * 128:(m + 1) * 128],
                                               rhs=QT[:, h, i0 * 128:(i1 + 1) * 128], start=True, stop=True)
                        return ins
                    S.op("pe", scm, reads=[("KTh", hs), ("QT", i0, h // 4), ("QT", i1, h // 4)], writes=[skey])
                    flush_pend()
                    pss = []
                    for k in range(ng):
                        m = m0 + k
                        ps_ = acnt["p"] % NPT
                        acnt["p"] += 1
                        pss.append(ps_)
                        c0 = 128 if m == qi1 else 0
                        S.op("act", lambda e, ps_=ps_, sb_=sb_, k=k, bs=bs, m=m, c0=c0: e.activation(
                            out=pT[ps_][:, c0:256], in_=sb_[:, k * 256 + c0:(k + 1) * 256], func=AF.Exp, scale=SC, bias=biasb[bs][:, m:m + 1]),
                            reads=[skey, ("biasb", bs)], writes=[("pT", ps_)])
                        if m == qi0:
                            S.op("pool", lambda e, ps_=ps_: e.tensor_tensor(out=pT[ps_][:, 0:128], in0=pT[ps_][:, 0:128], in1=C("UF"), op=ALU.mult),
                                 reads=[("pT", ps_), "cst"], writes=[("pT", ps_)])
                        if m == qi1:
                            S.op("pool", lambda e, ps_=ps_: e.tensor_tensor(out=pT[ps_][:, 128:256], in0=pT[ps_][:, 128:256], in1=C("UF"), op=ALU.mult),
                                 reads=[("pT", ps_), "cst"], writes=[("pT", ps_)])

                    def pvs(pss=pss, m0=m0, ng=ng, hs=hs, qi0=qi0, qi1=qi1):
                        def pvm(e):
                            ins = None
                            for k in range(ng):
                                m = m0 + k
                                if m <= qi0:
                                    ins = e.matmul(PB[4][:, 0:129], lhsT=pT[pss[k]][:, 0:128], rhs=Vh[hs][:, m, :], start=(m == 0), stop=(m == qi0))
                                ins = e.matmul(PB[5][:, 0:129], lhsT=pT[pss[k]][:, 128:256], rhs=Vh[hs][:, m, :], start=(m == 0), stop=(m == qi1))
                            return ins
                        S.op("pe", pvm, reads=[("pT", p_) for p_ in pss] + [("Vh", hs), ("Vh1", hs)], writes=[("PB", 4), ("PB", 5)])
                    pend.append(pvs)
                    m0 += ng

                def epi(h=h, i0=i0):
                    for j in range(2):
                        i = i0 + j
                        S.op("dve", lambda e, j=j: e.reciprocal(out=rden[j][:], in_=PB[4 + j][:, 128:129]),
                             reads=[("PB", 4 + j)], writes=[("rden", j)])
                        S.op("dve", lambda e, j=j: e.tensor_scalar(out=obb[j][:], in0=PB[4 + j][:, 0:128], scalar1=rden[j][:],
                                                                   scalar2=None, op0=ALU.mult),
                             reads=[("PB", 4 + j), ("rden", j)], writes=[("obb", j)])
                        transposes_to(obb[j], [("obb", j)], 1, lambda c0, n, i=i: obT[:, h:h + 1, i * 128:(i + 1) * 128], [("obT", h, i)], "act")
                pend.append(epi)
        flush_pend()

        p2.close()
        S.fence(lambda e: e.memset(CAR[:, 0, :], 0.0))
        p2b = contextlib.ExitStack()
        rdenb = [SB(p2b, "rdenb%d" % i, [128, 1], F32) for i in range(2)]
        CLF = SB(p2b, "CLF", [128, PT, 4, 8], F32)
        CKc = SB(p2b, "CKc", [128, PT, 4, 8], F32)
        CARc = SB(p2b, "CARc", [128, 32], F32)
        Rb = SB(p2b, "Rb", [128, 32], F32)
        Rn = SB(p2b, "Rn", [128, 32], F32)
        BN = SB(p2b, "BN", [128, 8], F32)
        Kc = [SB(p2b, "Kc%d" % i, [128, PT, 128], BF16) for i in range(2)]
        KcT = [SB(p2b, "KcT%d" % i, [128, PT * 128], BF16) for i in range(2)]
        Vc = [SB(p2b, "Vc%d" % i, [128, PT, 129], BF16) for i in range(2)]
        biasc = [SB(p2b, "biasc%d" % i, [128, PT], F32) for i in range(2)]
        Ec = [SB(p2b, "Ec%d" % i, [128, PT], F32) for i in range(2)]
        exs = [SB(p2b, "exs%d" % i, [128, 16, 32], F32) for i in range(2)]
        pTs = [SB(p2b, "pTs%d" % i, [128, PT, 32], BF16) for i in range(2)]
        KTn = SB(p2b, "KTn", [128, 8, 128], BF16)
        Vn = SB(p2b, "Vn", [128, 8, 129], BF16)
        pnf = SB(p2b, "pnf", [128, 128], F32)
        pnT = [SB(p2b, "pnT%d" % i, [128, 128], BF16) for i in range(2)]
        obs = [SB(p2b, "obs%d" % i, [32, 128], BF16) for i in range(2)]

        for i in range(2):
            S.op("pool", lambda e, i=i: e.memset(Vc[i][:, :, 128:129], 1.0), writes=[("Vc1", i)])
        S.op("pool", lambda e: e.memset(Vn[:, :, 128:129], 1.0), writes=["Vn1"])
        for r in range(4):
            S.op("sp", lambda e, r=r: e.dma_start(out=CLF[:, :, r, :], in_=clf_d[r].rearrange("(m p) h -> p m h", p=128)),
                 writes=[("CLF", r)], dma="CLF")
        clfk = [("CLF", r) for r in range(4)]
        for m in range(PT):
            if m > 0:
                S.op("pe", lambda e, m=m: e.matmul(PBC[:, 32:64], lhsT=C("SEL127"), rhs=CKc[:, m - 1, :, :].rearrange("p r h -> p (r h)"),
                                                   start=True, stop=True), reads=[("CKc", m - 1), "cst"], writes=[("PB", 2)])
                S.op("dve", lambda e: e.tensor_copy(out=CARc[:], in_=PBC[:, 32:64]), reads=[("PB", 2)], writes=["CARc"])
            S.op("pe", lambda e, m=m: e.matmul(PB[3][:, 64:96], lhsT=C("UF"), rhs=CLF[:, m, :, :].rearrange("p r h -> p (r h)"),
                                               start=True, stop=True), reads=clfk + ["cst"], writes=[("PB", 3)])
            if m > 0:
                S.op("dve", lambda e, m=m: e.tensor_tensor(out=CKc[:, m, :, :].rearrange("p r h -> p (r h)"), in0=PB[3][:, 64:96], in1=CARc[:], op=ALU.add),
                     reads=[("PB", 3), "CARc"], writes=[("CKc", m)])
            else:
                S.op("dve", lambda e, m=m: e.tensor_copy(out=CKc[:, m, :, :].rearrange("p r h -> p (r h)"), in_=PB[3][:, 64:96]),
                     reads=[("PB", 3)], writes=[("CKc", m)])
        def tot(e):
            ins = None
            for m in range(PT):
                ins = e.matmul(PBC[:, 96:128], lhsT=C("ALL1"), rhs=CLF[:, m, :, :].rearrange("p r h -> p (r h)"), start=(m == 0), stop=(m == PT - 1))
            return ins
        S.op("pe", tot, reads=clfk + ["cst"], writes=[("PB", 2)])
        S.op("dve", lambda e: e.tensor_copy(out=Rb[:], in_=PBC[:, 96:128]), reads=[("PB", 2)], writes=["Rb"])
        def newsum(e):
            ins = None
            for r in range(4):
                ins = e.matmul(PBC[:, 128 + 8 * r:136 + 8 * r], lhsT=C("SR4_%d" % r), rhs=LFB[:, NW, :], start=True, stop=True)
            return ins
        S.op("pe", newsum, reads=[("LFB", NW), "cst"], writes=[("PB", 2)])
        S.op("dve", lambda e: e.tensor_tensor(out=Rb[:], in0=PBC[:, 128:160], in1=Rb[:], op=ALU.add), reads=[("PB", 2), "Rb"], writes=["Rb"])
        S.op("pe", lambda e: e.matmul(PBC[:, 160:168], lhsT=C("SU4"), rhs=LFB[:, NW, :], start=True, stop=True),
             reads=[("LFB", NW), "cst"], writes=[("PB", 2)])
        S.op("dve", lambda e: e.tensor_copy(out=BN[:], in_=PBC[:, 160:168]), reads=[("PB", 2)], writes=["BN"])

        S.op("sp", lambda e: e.dma_start(out=KTn[:], in_=KTs[:, :, NW * 128:(NW + 1) * 128].rearrange("h d t -> d h t")),
             reads=[("KTs", NW, 0), ("KTs", NW, 1)], writes=["KTn"], dma="KTn")
        S.op("sp", lambda e: e.dma_start(out=Vn[:, :, 0:128], in_=Vs[NW * 128:(NW + 1) * 128, :].rearrange("p (h d) -> p h d", h=8)),
             reads=[("Vs", NW, 0), ("Vs", NW, 1)], writes=["Vn"], dma="Vn")

        PBSS = [PB[0], PB[1]]
        PBOS = [PB[4], PB[5]]
        PBN = PB[3]
        scnt = {"k": 0, "s": 0, "o": 0}
        s0 = TB * 128
        for h in range(8):
            hn = h % 2
            S.op("pe", lambda e, h=h: e.matmul(PBN[:, 0:128], lhsT=KTn[:, h, :], rhs=QT[:, h, s0:s0 + 128], start=True, stop=True),
                 reads=["KTn", ("QT", TB, h // 4)], writes=[("PB", 3)])
            S.op("act", lambda e, h=h: e.activation(out=pnf[:], in_=PBN[:, 0:128], func=AF.Exp, scale=SC, bias=BN[:, h:h + 1]),
                 reads=[("PB", 3), "BN"], writes=["pnf"])
            S.op("dve", lambda e, hn=hn: e.tensor_tensor(out=pnT[hn][:], in0=pnf[:], in1=C("U4"), op=ALU.mult),
                 reads=["pnf", "cst"], writes=[("pnT", hn)])
            for r in range(4):
                ks = scnt["k"] % 2
                scnt["k"] += 1
                S.op("pool", lambda e, ks=ks, r=r, h=h: e.dma_start(out=Kc[ks][:], in_=ck_d[r, :, h * 128:(h + 1) * 128].rearrange("(m p) d -> p m d", p=128)),
                     writes=[("Kc", ks)], dma=("Kc", ks))
                S.op("pool", lambda e, ks=ks, r=r, h=h: e.dma_start(out=Vc[ks][:, :, 0:128], in_=cv_d[r, :, h * 128:(h + 1) * 128].rearrange("(m p) d -> p m d", p=128)),
                     writes=[("Vc", ks)], dma=("Vc", ks))
                m0 = 0
                while m0 < PT:
                    n = min(8, PT - m0)
                    tb, tk = tbank()

                    def trk(e, tb=tb, ks=ks, m0=m0, n=n):
                        ins = None
                        for c in range(n):
                            ins = e.transpose(out=tb[:, c * 128:(c + 1) * 128], in_=Kc[ks][:, m0 + c, :], identity=idb[:])
                        return ins
                    S.op("pe", trk, reads=[("Kc", ks), "idb"], writes=[tk])
                    if (m0 // 8) % 2 == 0:
                        S.op("act", lambda e, tb=tb, ks=ks, m0=m0, n=n: e.activation(out=KcT[ks][:, m0 * 128:(m0 + n) * 128], in_=tb[:, 0:n * 128], func=AF.Copy),
                             reads=[tk], writes=[("KcT", ks, m0)])
                    else:
                        S.op("dve", lambda e, tb=tb, ks=ks, m0=m0, n=n: e.tensor_copy(out=KcT[ks][:, m0 * 128:(m0 + n) * 128], in_=tb[:, 0:n * 128]),
                             reads=[tk], writes=[("KcT", ks, m0)])
                    m0 += n
                S.op("dve", lambda e, ks=ks, r=r, h=h: e.tensor_scalar(out=biasc[ks][:], in0=CKc[:, :, r, h], scalar1=-1.0,
                                                                         scalar2=Rb[:, r * 8 + h:r * 8 + h + 1], op0=ALU.mult, op1=ALU.add),
                     reads=[("CKc", m) for m in range(PT)] + ["Rb"], writes=[("biasc", ks)])
                S.op("act", lambda e, ks=ks: e.activation(out=Ec[ks][:], in_=biasc[ks][:], func=AF.Exp), reads=[("biasc", ks)], writes=[("Ec", ks)])
                q0 = s0 + 32 * r
                m0 = 0
                while m0 < PT:
                    n = min(16, PT - m0)
                    sb_i = scnt["s"] % 2
                    scnt["s"] += 1
                    sbk = PBSS[sb_i]

                    def scm(e, sbk=sbk, ks=ks, m0=m0, n=n, h=h, q0=q0):
                        ins = None
                        for c in range(n):
                            ins = e.matmul(sbk[:, c * 32:(c + 1) * 32], lhsT=KcT[ks][:, (m0 + c) * 128:(m0 + c + 1) * 128],
                                           rhs=QT[:, h, q0:q0 + 32], start=True, stop=True)
                        return ins
                    S.op("pe", scm, reads=[("KcT", ks, (m0 // 8) * 8), ("KcT", ks, (m0 // 8) * 8 + 8), ("QT", TB, h // 4)], writes=[("PB", sb_i)])
                    S.op("act", lambda e, sbk=sbk, sb_i=sb_i, n=n: e.activation(out=exs[sb_i][:, 0:n, :].rearrange("p m t -> p (m t)"),
                                                                                in_=sbk[:, 0:n * 32], func=AF.Exp, scale=SC),
                         reads=[("PB", sb_i)], writes=[("exs", sb_i)])
                    S.op("dve", lambda e, sb_i=sb_i, ks=ks, m0=m0, n=n: e.tensor_tensor(
                        out=pTs[ks][:, m0:m0 + n, :], in0=exs[sb_i][:, 0:n, :],
                        in1=Ec[ks][:, m0:m0 + n].unsqueeze(2).to_broadcast([128, n, 32]), op=ALU.mult),
                        reads=[("exs", sb_i), ("Ec", ks)], writes=[("pTs", ks, m0)])
                    m0 += n
                ob_i = scnt["o"] % 2
                scnt["o"] += 1
                obk = PBOS[ob_i]

                def pv(e, obk=obk, ks=ks, hn=hn, r=r, h=h):
                    ins = None
                    for m in range(PT):
                        ins = e.matmul(obk[0:32, 0:129], lhsT=pTs[ks][:, m, :], rhs=Vc[ks][:, m, :], start=(m == 0), stop=False)
                    ins = e.matmul(obk[0:32, 0:129], lhsT=pnT[hn][:, 32 * r:32 * r + 32], rhs=Vn[:, h, :], start=False, stop=True)
                    return ins
                S.op("pe", pv, reads=[("pTs", ks, m0) for m0 in range(0, PT, 16)] + [("Vc", ks), ("Vc1", ks), ("pnT", hn), "Vn", "Vn1"],
                     writes=[("PB", 4 + ob_i)])
                S.op("dve", lambda e, ob_i=ob_i, obk=obk: e.reciprocal(out=rdenb[ob_i][0:32, :], in_=obk[0:32, 128:129]),
                     reads=[("PB", 4 + ob_i)], writes=[("rdenb", ob_i)])
                S.op("dve", lambda e, ob_i=ob_i, obk=obk: e.tensor_scalar(out=obs[ob_i][:], in0=obk[0:32, 0:128], scalar1=rdenb[ob_i][0:32, :],
                                                                           scalar2=None, op0=ALU.mult),
                     reads=[("PB", 4 + ob_i), ("rdenb", ob_i)], writes=[("obs", ob_i)])
                tb, tk = tbank()
                S.op("pe", lambda e, tb=tb, ob_i=ob_i: e.transpose(out=tb[:, 0:32], in_=obs[ob_i][:], identity=idb[0:32, 0:32]),
                     reads=[("obs", ob_i), "idb"], writes=[tk])
                S.op("act", lambda e, tb=tb, h=h, q0=q0: e.activation(out=obT[:, h, q0:q0 + 32], in_=tb[:, 0:32], func=AF.Copy),
                     reads=[tk], writes=[("obT", h, TB, r)])
        p2b.close()
        S.fence(lambda e: e.memset(CAR[:, 0, :], 0.0))

        p3 = contextlib.ExitStack()
        wpa = [SB(p3, "wpa%d" % i, [128, 8, PW], BF16) for i in range(2)]
        wpb = [SB(p3, "wpb%d" % i, [128, 8, PW], BF16) for i in range(2)]
        gab = [SB(p3, "gab%d" % i, [128, 2, PW], F32) for i in range(3)]
        t1 = [SB(p3, "t1_%d" % i, [128, PW], F32) for i in range(3)]
        t2 = [SB(p3, "t2_%d" % i, [128, PW], F32) for i in range(3)]
        mb = [SB(p3, "mb%d" % i, [128, PW], BF16) for i in range(3)]
        oakeys = lambda lt: [("oaT", h, lt) for h in range(8)]
        obkeys = lambda lt: ([("obT", h, lt) for h in range(8)] if lt < TB else [("obT", h, TB, r) for h in range(8) for r in range(4)])
        pend3 = []
        u = 0
        for n_ in range(NPW):
            ws = n_ % 2
            S.op("pool", lambda e, ws=ws, n_=n_: e.dma_start(out=wpa[ws][:], in_=w_pa_d[:, n_ * PW:(n_ + 1) * PW].rearrange("(c p) n -> p c n", p=128)),
                 writes=[("wpa", ws)], dma=("wpa", ws))
            S.op("pool", lambda e, ws=ws, n_=n_: e.dma_start(out=wpb[ws][:], in_=w_pb_d[:, n_ * PW:(n_ + 1) * PW].rearrange("(c p) n -> p c n", p=128)),
                 writes=[("wpb", ws)], dma=("wpb", ws))
            for lt in range(NTO):
                sl = u % 3
                pb2 = u % 2
                u += 1
                S.op("sp", lambda e, sl=sl, lt=lt, n_=n_: e.dma_start(
                    out=gab[sl][:], in_=Gs[lt * 128:(lt + 1) * 128, :].rearrange("p (g c) -> p g c", g=2)[:, :, n_ * PW:(n_ + 1) * PW]),
                    reads=[("Gs", lt, n_), ("Gs", lt, NPW + n_)], writes=[("gab", sl)], dma=("gab", sl))
                ba, bak = PB[pb2], ("PB", pb2)
                gemm_tile(lt, oaT, oakeys(lt), 8, wpa[ws], [("wpa", ws)], PW, ba, bak)
                bb, bbk = PB[2 + pb2], ("PB", 2 + pb2)
                gemm_tile(lt, obT, obkeys(lt), 8, wpb[ws], [("wpb", ws)], PW, bb, bbk)
                while pend3:
                    pend3.pop(0)()
                S.op("dve", lambda e, sl=sl, ba=ba: e.tensor_tensor(out=t1[sl][:], in0=ba[:, 0:PW], in1=gab[sl][:, 0, :], op=ALU.mult),
                     reads=[bak, ("gab", sl)], writes=[("t1", sl)])
                S.op("dve", lambda e, sl=sl, bb=bb: e.tensor_tensor(out=t2[sl][:], in0=bb[:, 0:PW], in1=gab[sl][:, 1, :], op=ALU.mult),
                     reads=[bbk, ("gab", sl)], writes=[("t2", sl)])
                S.op("dve", lambda e, sl=sl: e.tensor_tensor(out=mb[sl][:], in0=t1[sl][:], in1=t2[sl][:], op=ALU.add),
                     reads=[("t1", sl), ("t2", sl)], writes=[("mb", sl)])

                def trs(sl=sl, lt=lt, n_=n_):
                    transposes_to(mb[sl], [("mb", sl)], PWC, lambda c0, n: HT[:, n_ * PWC + c0:n_ * PWC + c0 + n, lt * 128:(lt + 1) * 128],
                                  [("HT", lt, "m", n_)], "act")
                pend3.append(trs)
        while pend3:
            pend3.pop(0)()
        p3.close()
        mid2.close()
        mid.close()
        S.fence(lambda e: e.memset(CAR[:, 0, :], 0.0))

        late = contextlib.ExitStack()
        X1 = SB(late, "X1", [128, NTO, D], F32)
        p4 = contextlib.ExitStack()
        wo = [SB(p4, "wo%d" % i, [128, KC, PW], BF16) for i in range(2)]
        xr = [SB(p4, "xr%d" % i, [128, PW], F32) for i in range(2)]
        mkeys = lambda lt: [("HT", lt, "m", n_) for n_ in range(NPW)]
        for n_ in range(NPW):
            ws = n_ % 2
            S.op("pool", lambda e, ws=ws, n_=n_: e.dma_start(out=wo[ws][:], in_=w_o_d[:, n_ * PW:(n_ + 1) * PW].rearrange("(c p) n -> p c n", p=128)),
                 writes=[("wo", ws)], dma=("wo", ws))
            for lt in range(NTO):
                sl = lt % 2
                src = xw[(3 * TB + lt) * 128:(3 * TB + lt + 1) * 128, n_ * PW:(n_ + 1) * PW] if lt < TB else xs[:, n_ * PW:(n_ + 1) * PW]
                S.op("sp", lambda e, sl=sl, src=src: e.dma_start(out=xr[sl][:], in_=src), writes=[("xr", sl)], dma=("xr", sl))
                bank, bkey = gbank()
                gemm_tile(lt, HT, mkeys(lt), KC, wo[ws], [("wo", ws)], PW, bank, bkey)
                S.op("dve", lambda e, sl=sl, bank=bank, lt=lt, n_=n_: e.tensor_tensor(out=X1[:, lt, n_ * PW:(n_ + 1) * PW], in0=bank[:, 0:PW],
                                                                                     in1=xr[sl][:], op=ALU.add),
                     reads=[bkey, ("xr", sl)], writes=[("X1", lt, n_)])
        p4.close()
        S.fence(lambda e: e.memset(CAR[:, 0, :], 0.0))
        p5 = contextlib.ExitStack()
        nss2 = [SB(p5, "nss2_%d" % i, [128, 1], F32) for i in range(2)]
        nrs2 = [SB(p5, "nrs2_%d" % i, [128, 1], F32) for i in range(2)]
        hb2 = [SB(p5, "hb2_%d" % i, [128, D], BF16) for i in range(2)]
        w1p = [SB(p5, "w1p%d" % i, [128, KC, FW], BF16) for i in range(2)]
        w2p = [SB(p5, "w2p%d" % i, [128, FWC, D], BF16) for i in range(2)]
        uT = [SB(p5, "uT%d" % i, [128, FWC, TOK], BF16) for i in range(2)]
        usq = [SB(p5, "usq%d" % i, [128, 512], F32) for i in range(2)]
        S.op("sp", lambda e: e.dma_start(out=gvec[:], in_=norm2_d.broadcast_to([128, D])),
             reads=[], writes=["gvec"], dma="gvec")
        x1keys = lambda lt: [("X1", lt, n_) for n_ in range(NPW)]
        for lt in range(NTO):
            sl = lt % 2
            rmsnorm_rstd(X1[:, lt, :], x1keys(lt), hb2[sl][:], nss2[sl][:], nrs2[sl][:], "m%d" % sl, D, junk_key=("hb2", sl))
            S.op("dve", lambda e, sl=sl, lt=lt: e.scalar_tensor_tensor(out=hb2[sl][:], in0=X1[:, lt, :], scalar=nrs2[sl][:], in1=gvec[:],
                                                                         op0=ALU.mult, op1=ALU.mult),
                 reads=x1keys(lt) + ["m%drstd" % sl, "gvec"], writes=[("hb2", sl)])
            transposes_to(hb2[sl], [("hb2", sl)], KC, lambda c0, n, lt=lt: HT[:, c0:c0 + n, lt * 128:(lt + 1) * 128],
                          [("HT", lt, "h2")], "act" if lt % 2 == 0 else "dve")

        tgroups = []
        t0 = 0
        while t0 < TOK:
            n = min(512, TOK - t0)
            tgroups.append((t0, n))
            t0 += n
        h2keys = [("HT", lt, "h2") for lt in range(NTO)]
        PBU = [PB[2], PB[3]]
        PBY = [PB[0], PB[1], PB[4], PB[5]]
        ucnt = {"u": 0, "y": 0}
        for fp in range(NFP):
            ws = fp % 2
            S.op("pool", lambda e, ws=ws, fp=fp: e.dma_start(out=w1p[ws][:], in_=w1_d[:, fp * FW:(fp + 1) * FW].rearrange("(c p) n -> p c n", p=128)),
                 writes=[("w1p", ws)], dma=("w1p", ws))
            S.op("pool", lambda e, ws=ws, fp=fp: e.dma_start(out=w2p[ws][:], in_=w2_d[fp * FW:(fp + 1) * FW, :].rearrange("(c p) n -> p c n", p=128)),
                 writes=[("w2p", ws)], dma=("w2p", ws))
            for j in range(FWC):
                for (t0, n) in tgroups:
                    ui = ucnt["u"] % 2
                    ucnt["u"] += 1
                    ub = PBU[ui]

                    def mm1(e, ub=ub, ws=ws, j=j, t0=t0, n=n):
                        ins = None
                        for c in range(KC):
                            ins = e.matmul(ub[:, 0:n], lhsT=w1p[ws][:, c, j * 128:(j + 1) * 128], rhs=HT[:, c, t0:t0 + n],
                                           start=(c == 0), stop=(c == KC - 1))
                        return ins
                    S.op("pe", mm1, reads=h2keys + [("w1p", ws)], writes=[("PB", 2 + ui)])
                    S.op("act", lambda e, ui=ui, ub=ub, n=n: e.activation(out=usq[ui][:, 0:n], in_=ub[:, 0:n], func=AF.Square),
                         reads=[("PB", 2 + ui)], writes=[("usq", ui)])
                    S.op("dve", lambda e, ui=ui, ub=ub, n=n, ws=ws, j=j, t0=t0: e.scalar_tensor_tensor(
                        out=uT[ws][:, j, t0:t0 + n], in0=ub[:, 0:n], scalar=0.0, in1=usq[ui][:, 0:n], op0=ALU.is_gt, op1=ALU.mult),
                        reads=[("PB", 2 + ui), ("usq", ui)], writes=[("uT", ws, j, t0)])
            ukeys = [("uT", ws, j, t0) for j in range(FWC) for (t0, n) in tgroups]
            for lt in range(NTO):
                for n_ in range(NPW):
                    yi = ucnt["y"] % 4
                    ucnt["y"] += 1
                    yb = PBY[yi]

                    def mm2(e, yb=yb, ws=ws, lt=lt, n_=n_):
                        ins = None
                        for c in range(FWC):
                            ins = e.matmul(yb[:, 0:PW], lhsT=uT[ws][:, c, lt * 128:(lt + 1) * 128], rhs=w2p[ws][:, c, n_ * PW:(n_ + 1) * PW],
                                           start=(c == 0), stop=(c == FWC - 1))
                        return ins
                    S.op("pe", mm2, reads=ukeys + [("w2p", ws)], writes=[("PB", (0, 1, 4, 5)[yi])])
                    S.op("dve", lambda e, yb=yb, lt=lt, n_=n_: e.tensor_tensor(out=X1[:, lt, n_ * PW:(n_ + 1) * PW], in0=yb[:, 0:PW],
                                                                               in1=X1[:, lt, n_ * PW:(n_ + 1) * PW], op=ALU.add),
                         reads=[("PB", (0, 1, 4, 5)[yi]), ("X1", lt, n_)], writes=[("X1", lt, n_)])

        S.op("sp", lambda e: e.dma_start(out=gvec[:], in_=normf_d.broadcast_to([128, D])), writes=["gvec"], dma="gvec")
        for lt in range(NTO):
            sl = lt % 2
            rmsnorm_rstd(X1[:, lt, :], x1keys(lt), hb2[sl][:], nss2[sl][:], nrs2[sl][:], "m%d" % sl, D, junk_key=("hb2", sl))
            S.op("dve", lambda e, sl=sl, lt=lt: e.scalar_tensor_tensor(out=X1[:, lt, :], in0=X1[:, lt, :], scalar=nrs2[sl][:], in1=gvec[:],
                                                                         op0=ALU.mult, op1=ALU.mult),
                 reads=x1keys(lt) + ["m%drstd" % sl, "gvec"], writes=x1keys(lt))
            S.op("sp", lambda e, sl=sl, lt=lt: e.dma_start(out=y_d[lt * 128:(lt + 1) * 128, :], in_=X1[:, lt, :]),
                 reads=x1keys(lt), dma=("ystD", sl))
        p5.close()
        late.close()

        finals = [s for s in ("SoD", ("SSoD", 0), ("SSoD", 1), ("kstD", 0), ("kstD", 1), "lfoD", ("ystD", 0), ("ystD", 1))]
        S.emit(outer, final_dma_slots=finals)
    return nc, S


_CACHE = {}


def _run(cfg, x_prompt, x_sample, cache_fox_k, cache_fox_v, cache_fox_logf, state_hgrn,
         norm1, w_in, b_fox_f, lb_logits, gnorm_a, w_pa, w_pb, w_o, norm2, w1, w2, norm_f):
    D, TB, PT = cfg["D"], cfg["TB"], cfg["PT"]
    NW = 4 * TB
    BLK = TB * 128
    NTO = TB + 1
    f = lambda a: np.ascontiguousarray(np.asarray(a, dtype=np.float32))
    x_prompt, x_sample = f(x_prompt), f(x_sample)
    ck, cv, clf, st0 = f(cache_fox_k)[0], f(cache_fox_v)[0], f(cache_fox_logf)[0], f(state_hgrn)[0]
    key = (D, TB, PT)
    if key not in _CACHE:
        _CACHE[key] = build(cfg)[0]
    nc = _CACHE[key]
    shared = {
        "norm1": f(norm1).reshape(1, D), "w_in": f(w_in)[0], "b_fox_f": f(b_fox_f).reshape(1, 8),
        "lb_logits": f(lb_logits), "gnorm_a": f(gnorm_a).reshape(1, 128), "w_pa": f(w_pa)[0], "w_pb": f(w_pb)[0],
        "w_o": f(w_o)[0], "norm2": f(norm2).reshape(1, D), "w1": f(w1)[0], "w2": f(w2)[0],
        "norm_f": f(norm_f).reshape(1, D), "consts": _CST,
    }
    in_maps = []
    for c in range(8):
        b, j = c // 4, c % 4
        xwin = np.zeros((NW * 128, D), np.float32)
        kmask = np.zeros((128, NW), np.float32)
        for p in range(4):
            src_blk = j - 3 + p
            if src_blk >= 0:
                xwin[p * BLK:(p + 1) * BLK] = x_prompt[b, src_blk * BLK:(src_blk + 1) * BLK]
            else:
                kmask[:, p * TB:(p + 1) * TB] = BIG
        m = dict(shared)
        m["xw"] = xwin
        m["xs"] = np.ascontiguousarray(x_sample[4 * c:4 * c + 4].reshape(128, D))
        m["kmask"] = kmask
        m["ck"] = np.ascontiguousarray(ck[4 * c:4 * c + 4].reshape(4, PT * 128, 1024))
        m["cv"] = np.ascontiguousarray(cv[4 * c:4 * c + 4].reshape(4, PT * 128, 1024))
        m["clf"] = np.ascontiguousarray(clf[4 * c:4 * c + 4])
        m["st0"] = np.ascontiguousarray(st0[4 * c:4 * c + 4])
        in_maps.append(m)
    res = run_bass_kernel_spmd(nc, in_maps, core_ids=list(range(8))).results
    SEQ = 4 * BLK
    y_p = np.zeros((2, SEQ, D), np.float32)
    y_s = np.zeros((32, 32, D), np.float32)
    k_p = np.zeros((1, 2, SEQ, 8, 128), np.float32)
    v_p = np.zeros((1, 2, SEQ, 8, 128), np.float32)
    lf_p = np.zeros((1, 2, SEQ, 8), np.float32)
    S_p = np.zeros((1, 2, 8, 128, 128), np.float32)
    k_s = np.zeros((1, 32, 32, 8, 128), np.float32)
    v_s = np.zeros((1, 32, 32, 8, 128), np.float32)
    lf_s = np.zeros((1, 32, 32, 8), np.float32)
    S_s = np.zeros((1, 32, 8, 128, 128), np.float32)
    for c in range(8):
        b, j = c // 4, c % 4
        r = res[c]
        sl = slice(j * BLK, (j + 1) * BLK)
        y_p[b, sl] = r["y"][:BLK]
        y_s[4 * c:4 * c + 4] = r["y"][BLK:].reshape(4, 32, D)
        k_p[0, b, sl] = r["ko"][:BLK].reshape(BLK, 8, 128)
        v_p[0, b, sl] = r["vo"][:BLK].reshape(BLK, 8, 128)
        lf_p[0, b, sl] = r["lfo"][:BLK]
        k_s[0, 4 * c:4 * c + 4] = r["ko"][BLK:].reshape(4, 32, 8, 128)
        v_s[0, 4 * c:4 * c + 4] = r["vo"][BLK:].reshape(4, 32, 8, 128)
        lf_s[0, 4 * c:4 * c + 4] = r["lfo"][BLK:].reshape(4, 32, 8)
        S_s[0, 4 * c:4 * c + 4] = r["Ss"]
        if j == 3:
            S_p[0, b] = r["So"]
    return (y_p, y_s, k_p, v_p, lf_p, S_p, k_s, v_s, lf_s, S_s)


def kernel(**inputs):
    cfg = {"D": 2048, "TB": 8, "PT": 32}
    return _run(cfg, **inputs)
```

```python
import contextlib
import numpy as np
import concourse.bass as bass
import concourse.mybir as mybir
from concourse.bass_utils import run_bass_kernel_spmd

F32 = mybir.dt.float32
BF16 = mybir.dt.bfloat16
AF = mybir.ActivationFunctionType
ALU = mybir.AluOpType

ENGS = ("pe", "act", "dve", "pool", "sp")
EPS = 1e-6
BIG = 30000.0


class _Op:
    __slots__ = ("eng", "fn", "reads", "writes", "dma", "deps", "signal", "tick", "sem")

    def __init__(self, eng, fn, reads, writes, dma):
        self.eng = eng
        self.fn = fn
        self.reads = reads
        self.writes = writes
        self.dma = dma
        self.deps = set()
        self.signal = False
        self.tick = 0
        self.sem = None


class Sched:
    def __init__(self, nc):
        self.nc = nc
        self.ops = []
        self.last_w = {}
        self.readers = {}
        self.fence_idx = None
        self.seen = set()

    def op(self, eng, fn, reads=(), writes=(), dma=None):
        o = _Op(eng, fn, tuple(reads), tuple(writes), dma)
        idx = len(self.ops)
        deps = o.deps
        for k in o.reads + o.writes:
            if k not in self.seen:
                self.seen.add(k)
                if self.fence_idx is not None:
                    deps.add(self.fence_idx)
        ops = self.ops

        def add(p, war):
            q = ops[p]
            if q.dma is None and dma is None and q.eng == eng:
                if eng == "pe" or war:
                    return
            deps.add(p)
        for k in o.reads:
            w = self.last_w.get(k)
            if w is not None:
                add(w, False)
            if isinstance(k, tuple) and k[0] in ("PB", "PTB"):
                for r in self.readers.get(k, ()):
                    if ops[r].eng != eng:
                        deps.add(r)
        for k in o.writes:
            w = self.last_w.get(k)
            if w is not None:
                add(w, False)
            for r in self.readers.get(k, ()):
                add(r, True)
        for k in o.reads:
            self.readers.setdefault(k, []).append(idx)
        for k in o.writes:
            self.last_w[k] = idx
            self.readers[k] = []
        deps.discard(idx)
        self.ops.append(o)
        return idx

    def fence(self, fn):
        o = _Op("pool", fn, (), (), None)
        idx = len(self.ops)
        for k, w in self.last_w.items():
            o.deps.add(w)
        for k, rs in self.readers.items():
            for r in rs:
                o.deps.add(r)
        if self.fence_idx is not None:
            o.deps.add(self.fence_idx)
        self.ops.append(o)
        self.fence_idx = idx
        allk = set(self.last_w) | set(self.readers)
        self.last_w = {k: idx for k in allk}
        self.readers = {k: [] for k in allk}
        return idx

    def emit(self, stack, final_dma_slots=()):
        nc = self.nc
        ops = self.ops
        for o in ops:
            for d in o.deps:
                ops[d].signal = True
        eng_sem = {e: stack.enter_context(nc.semaphore("s_" + e)) for e in ENGS}
        dma_sem, dma_cnt = {}, {}
        eng_cnt = {e: 0 for e in ENGS}
        for o in ops:
            if o.dma is not None:
                if o.dma not in dma_sem:
                    dma_sem[o.dma] = stack.enter_context(nc.semaphore("d_" + str(o.dma)))
                    dma_cnt[o.dma] = 0
                dma_cnt[o.dma] += 16
                o.tick = dma_cnt[o.dma]
                o.sem = dma_sem[o.dma]
                o.signal = True
            elif o.signal:
                eng_cnt[o.eng] += 1
                o.tick = eng_cnt[o.eng]
                o.sem = eng_sem[o.eng]
        for i in range(len(ops) - 2, -1, -1):
            o, nx = ops[i], ops[i + 1]
            if o.dma is not None and nx.dma == o.dma:
                o.tick = nx.tick
        self.n_sems = len(dma_sem) + len(ENGS)
        final = [(dma_sem[s], dma_cnt[s]) for s in final_dma_slots if s in dma_sem]

        def run_engine(ename, eng):
            known = {}
            for o in ops:
                if o.eng != ename:
                    continue
                need = {}
                for d in o.deps:
                    p = ops[d]
                    sid = id(p.sem)
                    if need.get(sid, (None, 0))[1] < p.tick:
                        need[sid] = (p.sem, p.tick)
                for sid, (sem, tick) in need.items():
                    if known.get(sid, 0) >= tick:
                        continue
                    eng.wait_ge(sem, tick)
                    known[sid] = tick
                ins = o.fn(eng)
                if o.signal:
                    assert ins is not None
                    ins.then_inc(o.sem, 16 if o.dma is not None else 1)
            if ename == "sp":
                for sem, cnt in final:
                    eng.wait_ge(sem, cnt)

        block = stack.enter_context(nc.Block())

        @block.tensor
        def _(eng):
            run_engine("pe", eng)

        @block.scalar
        def _(eng):
            run_engine("act", eng)

        @block.vector
        def _(eng):
            run_engine("dve", eng)

        @block.gpsimd
        def _(eng):
            run_engine("pool", eng)

        @block.sync
        def _(eng):
            run_engine("sp", eng)


def _consts():
    s = np.arange(128)[:, None]
    t = np.arange(128)[None, :]
    c = {}
    c["ID"] = (s == t)
    c["U2"] = (s <= t) & (s // 64 == t // 64)
    c["SU2"] = (s > t) & (s // 64 == t // 64)
    c["U4"] = (s <= t) & (s // 32 == t // 32)
    c["SU4"] = (s > t) & (s // 32 == t // 32)
    c["UF"] = (s <= t)
    c["SUF"] = (s > t)
    c["SEL127"] = (s == 127) & (t >= 0)
    c["ALL1"] = (s >= 0) & (t >= 0)
    for k in range(2):
        c["CM2_%d" % k] = (t // 64 == k) & (s >= 0)
    for k in range(4):
        c["CM4_%d" % k] = (t // 32 == k) & (s >= 0)
    for k in range(4):
        c["SR4_%d" % k] = (s // 32 == k) & (t >= 0)
    offs = {}
    cols = []
    o = 0
    for k, v in c.items():
        offs[k] = o
        cols.append(v.astype(np.float32))
        o += 128
    rm2 = np.zeros((128, 2), np.float32)
    rm2[np.arange(128), np.arange(128) // 64] = 1
    rm4 = np.zeros((128, 4), np.float32)
    rm4[np.arange(128), np.arange(128) // 32] = 1
    offs["RM2"] = o
    cols.append(rm2)
    o += 2
    offs["RM4"] = o
    cols.append(rm4)
    o += 4
    return np.ascontiguousarray(np.concatenate(cols, axis=1)), offs


_CST, _COFF = _consts()
NCST = _CST.shape[1]

AQ, AFo, AI, AG, BQ, BK, BV, BFL, GA = 0, 1024, 2048, 3072, 4096, 5120, 6144, 7168, 7176


def build(cfg):
    D = cfg["D"]
    TB = cfg["TB"]
    PT = cfg["PT"]
    KC = D // 128
    NW = 4 * TB
    NTO = TB + 1
    TOK = NTO * 128
    DFF = 4 * D
    NIN = GA + 2 * D
    PW = min(512, D)
    NPW = D // PW
    PWC = PW // 128
    FW = min(256, DFF)
    NFP = DFF // FW
    FWC = FW // 128
    SC = 128.0 ** -0.5

    nc = bass.Bass("TRN2", target_bir_lowering=False)

    def din(name, shape):
        return nc.dram_tensor(name, list(shape), F32, kind="ExternalInput").ap()

    def dout(name, shape):
        return nc.dram_tensor(name, list(shape), F32, kind="ExternalOutput").ap()

    xw = din("xw", [NW * 128, D])
    xs = din("xs", [128, D])
    kmask_d = din("kmask", [128, NW])
    ck_d = din("ck", [4, PT * 128, 1024])
    cv_d = din("cv", [4, PT * 128, 1024])
    clf_d = din("clf", [4, PT * 128, 8])
    st0_d = din("st0", [4, 8, 128, 128])
    norm1_d = din("norm1", [1, D])
    w_in_d = din("w_in", [D, NIN])
    bff_d = din("b_fox_f", [1, 8])
    lbl_d = din("lb_logits", [2, 1024])
    gn_d = din("gnorm_a", [1, 128])
    w_pa_d = din("w_pa", [1024, D])
    w_pb_d = din("w_pb", [1024, D])
    w_o_d = din("w_o", [D, D])
    norm2_d = din("norm2", [1, D])
    w1_d = din("w1", [D, DFF])
    w2_d = din("w2", [DFF, D])
    normf_d = din("norm_f", [1, D])
    cst_d = din("consts", [128, NCST])

    y_d = dout("y", [TOK, D])
    ko_d = dout("ko", [TOK, 1024])
    vo_d = dout("vo", [TOK, 1024])
    lfo_d = dout("lfo", [TOK, 8])
    So_d = dout("So", [8, 128, 128])
    Ss_d = dout("Ss", [4, 8, 128, 128])

    KTs = nc.dram_tensor("KTs", [8, 128, (NW + 1) * 128], BF16, kind="Internal").ap()
    Vs = nc.dram_tensor("Vs", [(NW + 1) * 128, 1024], BF16, kind="Internal").ap()
    Gs = nc.dram_tensor("Gs", [TOK, 2 * D], F32, kind="Internal").ap()

    S = Sched(nc)
    outer = contextlib.ExitStack()
    with outer:
        def SB(st, name, shape, dt):
            return st.enter_context(nc.sbuf_tensor(name, list(shape), dt))

        def PS(name, shape, dt):
            return outer.enter_context(nc.psum_tensor(name, list(shape), dt))

        PB = [PS("pb%d" % i, [128, 512], F32) for i in range(6)]
        PTB = [PS("ptb%d" % i, [128, 1024], BF16) for i in range(2)]
        cnt = {"g": 0, "t": 0}

        cst = SB(outer, "cst", [128, NCST], F32)
        idb = SB(outer, "idb", [128, 128], BF16)
        OML = SB(outer, "OML", [128, 1024], F32)
        GN = SB(outer, "GN", [128, 128], F32)
        BFF = SB(outer, "BFF", [128, 8], F32)
        gvec = SB(outer, "gvec", [128, D], F32)
        S32 = SB(outer, "S32", [128, 8, 128], F32)
        LFB = SB(outer, "LFB", [128, NW + 1, 8], F32)
        CKM = SB(outer, "CKM", [128, NW, 8], F32)
        CAR = SB(outer, "CAR", [128, NW + 1, 8], F32)
        KMK = SB(outer, "KMK", [128, NW], F32)
        wfl = SB(outer, "wfl", [128, KC, 8], BF16)
        HT = SB(outer, "HT", [128, KC, TOK], BF16)

        def C(name, w=128):
            o = _COFF[name]
            return cst[:, o:o + w]

        S.op("sp", lambda e: e.dma_start(out=cst[:], in_=cst_d), writes=["cst"], dma="cst")
        S.op("dve", lambda e: e.tensor_copy(out=idb[:], in_=C("ID")), reads=["cst"], writes=["idb"])
        S.op("sp", lambda e: e.dma_start(out=GN[:], in_=gn_d.broadcast_to([128, 128])), writes=["GN"], dma="GN")
        S.op("sp", lambda e: e.dma_start(out=BFF[:], in_=bff_d.broadcast_to([128, 8])), writes=["BFF"], dma="BFF")
        S.op("sp", lambda e: e.dma_start(out=KMK[:], in_=kmask_d), writes=["KMK"], dma="KMK")
        S.op("pool", lambda e: e.dma_start(out=wfl[:], in_=w_in_d[:, BFL:BFL + 8].rearrange("(c p) n -> p c n", p=128)),
             writes=["wfl"], dma="wfl")
        with contextlib.ExitStack() as st0:
            L0 = SB(st0, "L0", [128, 1024], F32)
            L1 = SB(st0, "L1", [128, 1024], F32)
            S.op("sp", lambda e: e.dma_start(out=L0[:], in_=lbl_d[0:1, :].broadcast_to([128, 1024])), writes=["L0"], dma="L0")
            S.op("sp", lambda e: e.dma_start(out=L1[:], in_=lbl_d[1:2, :].broadcast_to([128, 1024])), writes=["L1"], dma="L1")
            S.op("dve", lambda e: e.tensor_tensor(out=L0[:], in0=L0[:], in1=L1[:], op=ALU.subtract), reads=["L0", "L1"], writes=["L0"])
            S.op("act", lambda e: e.activation(out=L0[:], in_=L0[:], func=AF.Exp), reads=["L0"], writes=["L0"])
            S.op("dve", lambda e: e.tensor_scalar(out=L0[:], in0=L0[:], scalar1=1.0, scalar2=None, op0=ALU.add), reads=["L0"], writes=["L0"])
            S.op("dve", lambda e: e.reciprocal(out=OML[:], in_=L0[:]), reads=["L0"], writes=["OML"])
        S.op("pool", lambda e: e.memset(S32[:], 0.0), writes=["S32"])
        S.op("pool", lambda e: e.memset(CAR[:, 0, :], 0.0), writes=[("CAR", 0)])

        def gbank():
            i = cnt["g"] % 2
            cnt["g"] += 1
            return PB[i], ("PB", i)

        def tbank():
            i = cnt["t"] % 2
            cnt["t"] += 1
            return PTB[i], ("PTB", i)

        def rmsnorm_rstd(src_ap, src_keys, junk, ss, rstd, tag, n, junk_key=None):
            S.op("act", lambda e: e.activation(out=junk, in_=src_ap, func=AF.Square, accum_out=ss),
                 reads=src_keys, writes=[junk_key if junk_key is not None else tag + "junk", tag + "ss"])
            S.op("act", lambda e: e.activation(out=rstd, in_=ss, func=AF.Ln, scale=1.0 / n, bias=EPS),
                 reads=[tag + "ss"], writes=[tag + "rstd"])
            S.op("act", lambda e: e.activation(out=rstd, in_=rstd, func=AF.Exp, scale=-0.5),
                 reads=[tag + "rstd"], writes=[tag + "rstd"])

        def transposes_to(src_bf, src_keys, nchunk, dst_fn, dst_keys, evac_eng):
            c0 = 0
            while c0 < nchunk:
                n = min(8, nchunk - c0)
                tb, tk = tbank()

                def tr(e, c0=c0, n=n, tb=tb):
                    ins = None
                    for c in range(n):
                        ins = e.transpose(out=tb[:, c * 128:(c + 1) * 128], in_=src_bf[:, (c0 + c) * 128:(c0 + c + 1) * 128],
                                          identity=idb[:])
                    return ins
                S.op("pe", tr, reads=list(src_keys) + ["idb"], writes=[tk])
                dst = dst_fn(c0, n)
                src = tb[:, 0:n * 128].rearrange("p (c t) -> p c t", c=n)
                if evac_eng == "act":
                    S.op("act", lambda e, dst=dst, src=src: e.activation(out=dst, in_=src, func=AF.Copy), reads=[tk], writes=dst_keys)
                else:
                    S.op("dve", lambda e, dst=dst, src=src: e.tensor_copy(out=dst, in_=src), reads=[tk], writes=dst_keys)
                c0 += n

        def gemm_tile(lt, act_T, act_keys, kchunks, wp_ap, wp_keys, ncols, bank, bkey):
            def mm(e):
                ins = None
                for c in range(kchunks):
                    ins = e.matmul(bank[:, 0:ncols], lhsT=act_T[:, c, lt * 128:(lt + 1) * 128], rhs=wp_ap[:, c, 0:ncols],
                                   start=(c == 0), stop=(c == kchunks - 1))
                return ins
            S.op("pe", mm, reads=list(act_keys) + list(wp_keys), writes=[bkey])

        mid = contextlib.ExitStack()
        QT = SB(mid, "QT", [128, 8, TOK], BF16)
        oaT = SB(mid, "oaT", [128, 8, TOK], BF16)

        p1 = contextlib.ExitStack()
        wp = [SB(p1, "wp%d" % i, [128, KC, 512], BF16) for i in range(2)]
        xt = [SB(p1, "xt%d" % i, [128, D], F32) for i in range(2)]
        hb = [SB(p1, "hb%d" % i, [128, D], BF16) for i in range(1)] * 2
        nss = [SB(p1, "nss%d" % i, [128, 1], F32) for i in range(2)]
        nrs = [SB(p1, "nrs%d" % i, [128, 1], F32) for i in range(2)]
        NSET = 4
        hg = []
        for i in range(NSET):
            d = {}
            d["E3"] = SB(p1, "E3_%d" % i, [128, 384], F32)
            d["T3"] = SB(p1, "T3_%d" % i, [128, 384], F32)
            d["SL"] = SB(p1, "SL_%d" % i, [128, 384], F32)
            d["u"] = SB(p1, "u_%d" % i, [128, 128], F32)
            d["ka"] = SB(p1, "ka_%d" % i, [128, 128], F32)
            d["ebT"] = SB(p1, "ebT_%d" % i, [128, 4], F32)
            d["vb"] = SB(p1, "vb_%d" % i, [128, 128], BF16)
            d["qt"] = SB(p1, "qt_%d" % i, [128, 256], BF16)
            d["kp"] = SB(p1, "kp_%d" % i, [128, 128], BF16)
            d["qkT"] = SB(p1, "qkT_%d" % i, [128, 2, 128], BF16)
            d["atb"] = SB(p1, "atb_%d" % i, [128, 128], BF16)
            d["gs"] = SB(p1, "gs_%d" % i, [128, 128], F32)
            d["oss"] = SB(p1, "oss_%d" % i, [128, 1], F32)
            d["ors"] = SB(p1, "ors_%d" % i, [128, 1], F32)
            d["oa"] = SB(p1, "oa_%d" % i, [128, 128], BF16)
            d["eb"] = d["E3"][:, 0:128]
            d["enb"] = d["E3"][:, 128:256]
            d["ekp"] = d["E3"][:, 256:384]
            d["lf"] = d["u"]
            d["oj"] = d["T3"][:, 0:128]
            hg.append(d)
        smp_qTc = SB(p1, "smp_qTc", [128, 4, 128], BF16)
        smp_vc = SB(p1, "smp_vc", [128, 4, 128], BF16)
        for d in hg:
            d["qTc"] = smp_qTc
            d["vc"] = smp_vc
        Sring = [SB(p1, "Sring%d" % i, [128, 128], BF16) for i in range(8)]
        SS32 = [SB(p1, "SS32_%d" % i, [128, 4, 128], F32) for i in range(1)] * 2
        SSbf = [SB(p1, "SSbf_%d" % i, [128, 4, 128], BF16) for i in range(1)] * 2
        SSo = [SB(p1, "SSo_%d" % i, [128, 4, 128], F32) for i in range(1)] * 2
        kst = [SB(p1, "kst%d" % i, [128, 512], F32) for i in range(2)]
        kbf = [SB(p1, "kbf%d" % i, [128, 512], BF16) for i in range(2)]
        KTst = [SB(p1, "KTst%d" % i, [128, 4, 128], BF16) for i in range(2)]
        fz = [SB(p1, "fz%d" % i, [128, 8], F32) for i in range(2)]
        gE = kst

        PBM, PBAT, PBSU, PBO = PB[2], PB[3], PB[4], PB[5]
        hcnt = {"n": 0}

        def hgrn_item(blk, lt, h, mode, bank, bkey, tokcol):
            si = hcnt["n"] % NSET
            hcnt["n"] += 1
            d = hg[si]
            _al = {"eb": "E3", "enb": "E3", "ekp": "E3", "lf": "u", "oj": "T3"}
            K = lambda n: ("hg", si, _al.get(n, n)) if n not in ("qTc", "vc") else ("smp", n)
            own = mode != "pre"
            nch = 4 if mode == "smp" else 2
            Uc, SUc = ("U4", "SU4") if mode == "smp" else ("U2", "SU2")
            RMo = _COFF["RM4"] if mode == "smp" else _COFF["RM2"]
            CMn = "CM4_%d" if mode == "smp" else "CM2_%d"
            if own:
                fo, io, qo, go = 128, 384, 0, 256
                ne = 384
            else:
                fo, io = 0, 128
                ne = 128
            E3, T3, SL = d["E3"], d["T3"], d["SL"]
            S.op("act", lambda e: e.activation(out=E3[:, 0:ne], in_=bank[:, 0:ne], func=AF.Exp, scale=-1.0), reads=[bkey], writes=[K("E3")])
            S.op("act", lambda e: e.activation(out=d["vb"][:], in_=bank[:, io:io + 128], func=AF.Copy), reads=[bkey], writes=[K("vb")])
            S.op("act", lambda e: e.activation(out=T3[:, 0:ne], in_=E3[:, 0:ne], func=AF.Ln, bias=1.0), reads=[K("E3")], writes=[K("T3")])
            S.op("act", lambda e: e.activation(out=T3[:, 0:ne], in_=T3[:, 0:ne], func=AF.Exp, scale=-1.0), reads=[K("T3")], writes=[K("T3")])
            if own:
                S.op("dve", lambda e: e.tensor_tensor(out=SL[:], in0=bank[:, 0:384], in1=T3[:], op=ALU.mult),
                     reads=[bkey, K("T3")], writes=[K("SL")])
            S.op("dve", lambda e: e.tensor_tensor(out=d["u"][:], in0=E3[:, fo:fo + 128], in1=T3[:, fo:fo + 128], op=ALU.mult),
                 reads=[K("E3"), K("T3")], writes=[K("u")])
            S.op("pool", lambda e: e.tensor_tensor(out=d["ka"][:], in0=d["u"][:], in1=OML[:, h * 128:(h + 1) * 128], op=ALU.mult),
                 reads=[K("u"), "OML"], writes=[K("ka")])
            S.op("act", lambda e: e.activation(out=d["lf"][:], in_=d["ka"][:], func=AF.Ln, scale=-1.0, bias=1.0),
                 reads=[K("ka")], writes=[K("lf")])
            def cum(e):
                ins = None
                if own:
                    ins = e.matmul(PBM[:, 0:128], lhsT=C(Uc), rhs=d["lf"][:], start=True, stop=True)
                ins = e.matmul(PBM[:, 128:256], lhsT=C(SUc), rhs=d["lf"][:], start=True, stop=True)
                ins = e.matmul(PBM[:, 256:256 + nch], lhsT=d["lf"][:], rhs=cst[:, RMo:RMo + nch], start=True, stop=True)
                return ins
            S.op("pe", cum, reads=[K("lf"), "cst"], writes=[("PB", 2)])
            if own:
                S.op("act", lambda e: e.activation(out=d["eb"][:], in_=PBM[:, 0:128], func=AF.Exp), reads=[("PB", 2)], writes=[K("eb")])
                S.op("act", lambda e: e.activation(out=d["enb"][:], in_=PBM[:, 0:128], func=AF.Exp, scale=-1.0), reads=[("PB", 2)], writes=[K("enb")])
            S.op("act", lambda e: e.activation(out=d["ekp"][:], in_=PBM[:, 128:256], func=AF.Exp), reads=[("PB", 2)], writes=[K("ekp")])
            S.op("act", lambda e: e.activation(out=d["ebT"][:, 0:nch], in_=PBM[:, 256:256 + nch], func=AF.Exp), reads=[("PB", 2)], writes=[K("ebT")])
            S.op("dve", lambda e: e.tensor_tensor(out=d["kp"][:], in0=d["ka"][:], in1=d["ekp"][:], op=ALU.mult),
                 reads=[K("ka"), K("ekp")], writes=[K("kp")])
            for c in range(nch):
                S.op("pool", lambda e, c=c: e.tensor_scalar(out=d["vc"][:, c, :], in0=d["vb"][:], scalar1=cst[:, RMo + c:RMo + c + 1],
                                                            scalar2=1.0, op0=ALU.mult, op1=ALU.mult),
                     reads=[K("vb"), "cst"], writes=[K("vc", ) + (c,)])
            if own:
                S.op("dve", lambda e: e.tensor_tensor(out=d["qt"][:, 0:128], in0=SL[:, 0:128], in1=d["eb"][:], op=ALU.mult),
                     reads=[K("SL"), K("eb")], writes=[K("qt0")])
                S.op("dve", lambda e: e.tensor_tensor(out=d["qt"][:, 128:256], in0=d["ka"][:], in1=d["enb"][:], op=ALU.mult),
                     reads=[K("ka"), K("enb")], writes=[K("qt1")])
                S.op("pool", lambda e: e.tensor_tensor(out=d["gs"][:], in0=SL[:, 256:384], in1=GN[:], op=ALU.mult),
                     reads=[K("SL"), "GN"], writes=[K("gs")])
                transposes_to(d["qt"], [K("qt0"), K("qt1")], 2, lambda c0, n: d["qkT"][:, c0:c0 + n, :], [K("qkT")], "act")
                S.op("pe", lambda e: e.matmul(PBAT[:, 0:128], lhsT=d["qkT"][:, 1, :], rhs=d["qkT"][:, 0, :], start=True, stop=True),
                     reads=[K("qkT")], writes=[("PB", 3)])
                S.op("dve", lambda e: e.tensor_tensor(out=d["atb"][:], in0=PBAT[:, 0:128], in1=C(Uc), op=ALU.mult),
                     reads=[("PB", 3), "cst"], writes=[K("atb")])
                for c in range(nch):
                    S.op("pool", lambda e, c=c: e.tensor_tensor(out=d["qTc"][:, c, :], in0=d["qkT"][:, 0, :], in1=C(CMn % c), op=ALU.mult),
                         reads=[K("qkT"), "cst"], writes=[K("qTc") + (c,)])
            if mode == "smp":
                sslot = 0
                S.op("sp", lambda e: e.dma_start(out=SS32[sslot][:], in_=st0_d[:, h, :, :].rearrange("r d e -> d r e")),
                     writes=[("SS32", sslot)], dma=("SS32", sslot))
                S.op("act", lambda e: e.activation(out=SSbf[sslot][:], in_=SS32[sslot][:], func=AF.Copy),
                     reads=[("SS32", sslot)], writes=[("SSbf", sslot)])
            for c in range(nch):
                if own:
                    if mode == "smp":
                        rhs_state, skey = SSbf[0][:, c, :], ("SSbf", 0)
                    else:
                        rhs_state, skey = Sbf[c % 2][:, h, :], ("Sbf", c % 2, h)
                    S.op("pe", lambda e, c=c, rhs_state=rhs_state: e.matmul(PBO[:, 0:128], lhsT=d["qTc"][:, c, :], rhs=rhs_state,
                                                                            start=(c == 0), stop=False),
                         reads=[K("qTc") + (c,), skey], writes=[("PB", 5)])
                S.op("pe", lambda e, c=c: e.matmul(PBSU[:, 0:128], lhsT=d["kp"][:], rhs=d["vc"][:, c, :], start=True, stop=True),
                     reads=[K("kp"), K("vc") + (c,)], writes=[("PB", 4)])
                if mode == "smp":
                    sslot = 0
                    S.op("dve", lambda e, c=c, sslot=sslot: e.scalar_tensor_tensor(
                        out=SSo[sslot][:, c, :], in0=SS32[sslot][:, c, :], scalar=d["ebT"][:, c:c + 1], in1=PBSU[:, 0:128],
                        op0=ALU.mult, op1=ALU.add), reads=[("SS32", sslot), K("ebT"), ("PB", 4)], writes=[("SSo", sslot, c)])
                else:
                    S.op("dve", lambda e, c=c: e.scalar_tensor_tensor(
                        out=S32[:, h, :], in0=S32[:, h, :], scalar=d["ebT"][:, c:c + 1], in1=PBSU[:, 0:128],
                        op0=ALU.mult, op1=ALU.add), reads=[("S32", h), K("ebT"), ("PB", 4)], writes=[("S32", h)])
                    if blk == 3 or (blk == 2 and lt == TB - 1 and c == nch - 1):
                        nslot = (c + 1) % 2
                        S.op("act", lambda e, nslot=nslot: e.activation(out=Sbf[nslot][:, h, :], in_=S32[:, h, :], func=AF.Copy),
                             reads=[("S32", h)], writes=[("Sbf", nslot, h)])
            if mode == "smp":
                sslot = 0
                S.op("sp", lambda e: e.dma_start(out=Ss_d[:, h, :, :].rearrange("r d e -> d r e"), in_=SSo[sslot][:]),
                     reads=[("SSo", sslot, c) for c in range(4)], dma=("SSoD", sslot))
            if own:
                S.op("pe", lambda e: e.matmul(PBO[:, 0:128], lhsT=d["atb"][:], rhs=d["vb"][:], start=False, stop=True),
                     reads=[K("atb"), K("vb")], writes=[("PB", 5)])
                rmsnorm_rstd(PBO[:, 0:128], [("PB", 5)], d["oj"], d["oss"][:], d["ors"][:], "hg%d" % si, 128, junk_key=K("T3"))
                S.op("dve", lambda e: e.scalar_tensor_tensor(out=d["oa"][:], in0=PBO[:, 0:128], scalar=d["ors"][:], in1=d["gs"][:],
                                                             op0=ALU.mult, op1=ALU.mult),
                     reads=[("PB", 5), "hg%drstd" % si, K("gs")], writes=[K("oa")])
                transposes_to(d["oa"], [K("oa")], 1, lambda c0, n: oaT[:, h:h + 1, tokcol:tokcol + 128], [("oaT", h, lt)], "dve")

        gcnt = {"n": 0, "m": 0, "su": 0}

        def hgrn_gen(blk, lt, h, mode, wp_ap, wkeys, ncols, ring0):
            si = gcnt["n"] % NSET
            gcnt["n"] += 1
            d = hg[si]
            _al = {"eb": "E3", "enb": "E3", "ekp": "E3", "lf": "u", "oj": "T3"}
            K = lambda n: ("hg", si, _al.get(n, n))
            own = mode == "own"
            E3, T3, SL = d["E3"], d["T3"], d["SL"]
            bank, bkey = gbank()
            gemm_tile(lt, HT, [("HT", lt)], KC, wp_ap, wkeys, ncols, bank, bkey)
            yield
            mi = 2 + gcnt["m"] % 2
            gcnt["m"] += 1
            PM, pmk = PB[mi], ("PB", mi)
            if own:
                fo, io = 128, 384
                S.op("act", lambda e: e.activation(out=E3[:], in_=bank[:, 0:384], func=AF.Exp, scale=-1.0), reads=[bkey], writes=[K("E3")])
                S.op("act", lambda e: e.activation(out=d["vb"][:], in_=bank[:, io:io + 128], func=AF.Copy), reads=[bkey], writes=[K("vb")])
                S.op("act", lambda e: e.activation(out=T3[:], in_=E3[:], func=AF.Ln, bias=1.0), reads=[K("E3")], writes=[K("T3")])
                S.op("act", lambda e: e.activation(out=T3[:], in_=T3[:], func=AF.Exp, scale=-1.0), reads=[K("T3")], writes=[K("T3")])
                yield
                S.op("dve", lambda e: e.tensor_tensor(out=d["u"][:], in0=E3[:, fo:fo + 128], in1=T3[:, fo:fo + 128], op=ALU.mult),
                     reads=[K("E3"), K("T3")], writes=[K("u")])
                S.op("dve", lambda e: e.tensor_tensor(out=d["ka"][:], in0=d["u"][:], in1=OML[:, h * 128:(h + 1) * 128], op=ALU.mult),
                     reads=[K("u"), "OML"], writes=[K("ka")])
                S.op("dve", lambda e: e.tensor_tensor(out=SL[:], in0=bank[:, 0:384], in1=T3[:], op=ALU.mult),
                     reads=[bkey, K("T3")], writes=[K("SL")])
                S.op("pool", lambda e: e.tensor_tensor(out=d["gs"][:], in0=SL[:, 256:384], in1=GN[:], op=ALU.mult),
                     reads=[K("SL"), "GN"], writes=[K("gs")])
            else:
                fo, io = 0, 128
                S.op("act", lambda e: e.activation(out=E3[:, 0:128], in_=bank[:, 0:128], func=AF.Exp), reads=[bkey], writes=[K("E3")])
                S.op("act", lambda e: e.activation(out=d["vb"][:], in_=bank[:, io:io + 128], func=AF.Copy), reads=[bkey], writes=[K("vb")])
                yield
                S.op("dve", lambda e: e.tensor_scalar(out=T3[:, 0:128], in0=E3[:, 0:128], scalar1=1.0, scalar2=None, op0=ALU.add),
                     reads=[K("E3")], writes=[K("T3")])
                S.op("dve", lambda e: e.reciprocal(out=T3[:, 0:128], in_=T3[:, 0:128]), reads=[K("T3")], writes=[K("T3")])
                S.op("dve", lambda e: e.tensor_tensor(out=d["ka"][:], in0=T3[:, 0:128], in1=OML[:, h * 128:(h + 1) * 128], op=ALU.mult),
                     reads=[K("T3"), "OML"], writes=[K("ka")])
            yield
            S.op("act", lambda e: e.activation(out=d["lf"][:], in_=d["ka"][:], func=AF.Ln, scale=-1.0, bias=1.0), reads=[K("ka")], writes=[K("lf")])
            yield
            RMo = _COFF["RM2"]

            def cum(e):
                ins = None
                if own:
                    ins = e.matmul(PM[:, 0:128], lhsT=C("U2"), rhs=d["lf"][:], start=True, stop=True)
                ins = e.matmul(PM[:, 128:256], lhsT=C("SU2"), rhs=d["lf"][:], start=True, stop=True)
                ins = e.matmul(PM[:, 256:258], lhsT=d["lf"][:], rhs=cst[:, RMo:RMo + 2], start=True, stop=True)
                return ins
            S.op("pe", cum, reads=[K("lf"), "cst"], writes=[pmk])
            yield
            if own:
                S.op("act", lambda e: e.activation(out=d["eb"], in_=PM[:, 0:128], func=AF.Exp), reads=[pmk], writes=[K("eb")])
                S.op("act", lambda e: e.activation(out=d["enb"], in_=PM[:, 0:128], func=AF.Exp, scale=-1.0), reads=[pmk], writes=[K("enb")])
            S.op("act", lambda e: e.activation(out=d["ekp"], in_=PM[:, 128:256], func=AF.Exp), reads=[pmk], writes=[K("ekp")])
            S.op("act", lambda e: e.activation(out=d["ebT"][:, 0:2], in_=PM[:, 256:258], func=AF.Exp), reads=[pmk], writes=[K("ebT")])
            yield
            S.op("dve", lambda e: e.tensor_tensor(out=d["kp"][:], in0=d["ka"][:], in1=d["ekp"], op=ALU.mult),
                 reads=[K("ka"), K("ekp")], writes=[K("kp")])
            if own:
                S.op("dve", lambda e: e.tensor_tensor(out=d["qt"][:, 0:128], in0=SL[:, 0:128], in1=d["eb"], op=ALU.mult),
                     reads=[K("SL"), K("eb")], writes=[K("qt0")])
                S.op("dve", lambda e: e.tensor_tensor(out=d["qt"][:, 128:256], in0=d["ka"][:], in1=d["enb"], op=ALU.mult),
                     reads=[K("ka"), K("enb")], writes=[K("qt1")])
                yield
                transposes_to(d["qt"], [K("qt0"), K("qt1")], 2, lambda c0, n: d["qkT"][:, c0:c0 + n, :], [K("qkT")], "act")
            yield
            if own:
                vc2 = d["u"][:].bitcast(BF16).rearrange("p (c e) -> p c e", c=2)
                RMo2 = _COFF["RM2"]
                for c in range(2):
                    S.op("pool", lambda e, c=c: e.tensor_scalar(out=vc2[:, c, :], in0=d["vb"][:], scalar1=cst[:, RMo2 + c:RMo2 + c + 1],
                                                                scalar2=1.0, op0=ALU.mult, op1=ALU.mult),
                         reads=[K("vb"), "cst"], writes=[K("u")])
                PSU, psk = PB[4], ("PB", 4)
                PSU1, psk1 = PB[4], ("PB", 4)
                su1_ap = PSU[:, 256:384]

                def su(e):
                    e.matmul(PSU[:, 0:128], lhsT=d["qkT"][:, 1, :], rhs=d["qkT"][:, 0, :], start=True, stop=True)
                    e.matmul(PSU[:, 128:256], lhsT=d["kp"][:], rhs=vc2[:, 0, :], start=True, stop=True)
                    return e.matmul(PSU[:, 256:384], lhsT=d["kp"][:], rhs=vc2[:, 1, :], start=True, stop=True)
                S.op("pe", su, reads=[K("kp"), K("u"), K("qkT")], writes=[psk])
            else:
                PSU, psk = PB[4], ("PB", 4)
                PSU1, psk1 = PB[5], ("PB", 5)
                su1_ap = PSU1[:, 0:128]

                def su(e):
                    e.matmul(PSU[:, 128:256], lhsT=d["kp"][0:64, :], rhs=d["vb"][0:64, :], start=True, stop=True)
                    return e.matmul(PSU1[:, 0:128], lhsT=d["kp"][64:128, :], rhs=d["vb"][64:128, :], start=True, stop=True)
                S.op("pe", su, reads=[K("kp"), K("vb")], writes=[psk, psk1])
            yield
            if own:
                S.op("dve", lambda e: e.tensor_tensor(out=d["atb"][:], in0=PSU[:, 0:128], in1=C("U2"), op=ALU.mult),
                     reads=[psk, "cst"], writes=[K("atb")])
            if own and lt == 0:
                S.op("act", lambda e: e.activation(out=Sring[ring0 % 8][:], in_=S32[:, h, :], func=AF.Copy),
                     reads=[("S32", h)], writes=[("Sring", ring0 % 8)])
            for c in range(2):
                src_ap, src_k = (PSU[:, 128:256], psk) if c == 0 else (su1_ap, psk1)
                S.op("dve", lambda e, c=c, src_ap=src_ap: e.scalar_tensor_tensor(
                    out=S32[:, h, :], in0=S32[:, h, :], scalar=d["ebT"][:, c:c + 1], in1=src_ap,
                    op0=ALU.mult, op1=ALU.add), reads=[("S32", h), K("ebT"), src_k], writes=[("S32", h)])
                if own and not (lt == TB - 1 and c == 1):
                    rs = (ring0 + c + 1) % 8
                    S.op("act", lambda e, rs=rs: e.activation(out=Sring[rs][:], in_=S32[:, h, :], func=AF.Copy),
                         reads=[("S32", h)], writes=[("Sring", rs)])
            if not own:
                return
            yield
            ra, rb = ring0 % 8, (ring0 + 1) % 8

            def omm(e):
                e.matmul(PB[5][0:64, 0:128], lhsT=d["qkT"][:, 0, 0:64], rhs=Sring[ra][:], start=True, stop=False)
                e.matmul(PB[5][0:64, 0:128], lhsT=d["atb"][:, 0:64], rhs=d["vb"][:], start=False, stop=True)
                e.matmul(PB[5][64:128, 0:128], lhsT=d["qkT"][:, 0, 64:128], rhs=Sring[rb][:], start=True, stop=False)
                return e.matmul(PB[5][64:128, 0:128], lhsT=d["atb"][:, 64:128], rhs=d["vb"][:], start=False, stop=True)
            S.op("pe", omm, reads=[K("qkT"), K("atb"), K("vb"), ("Sring", ra), ("Sring", rb)], writes=[("PB", 5)])
            yield
            o32 = SL[:, 0:128]
            S.op("act", lambda e: e.activation(out=o32, in_=PB[5][:, 0:128], func=AF.Copy), reads=[("PB", 5)], writes=[K("SL")])
            rmsnorm_rstd(PB[5][:, 0:128], [("PB", 5)], d["oj"], d["oss"][:], d["ors"][:], "hg%d" % si, 128, junk_key=K("T3"))
            S.op("dve", lambda e: e.scalar_tensor_tensor(out=d["oa"][:], in0=o32, scalar=d["ors"][:], in1=d["gs"][:],
                                                         op0=ALU.mult, op1=ALU.mult),
                 reads=[K("SL"), "hg%drstd" % si, K("gs")], writes=[K("oa")])
            yield
            transposes_to(d["oa"], [K("oa")], 1, lambda c0, n: oaT[:, h:h + 1, lt * 128:(lt + 1) * 128], [("oaT", h, lt)], "dve")

        def hgrn_pre2_gen(lt, h0, wp_ap, wkeys):
            si = gcnt["n"] % NSET
            gcnt["n"] += 1
            d = hg[si]
            K = lambda n: ("hg", si, n)
            E = d["E3"][:, 0:256]
            T = d["T3"][:, 0:256]
            ka = d["SL"][:, 0:256]
            lf = d["E3"][:, 0:256]
            ekp = d["T3"][:, 0:256]
            vb2 = d["qt"]
            kp2 = d["qkT"][:].rearrange("p a b -> p (a b)")
            bank, bkey = gbank()
            gemm_tile(lt, HT, [("HT", lt)], KC, wp_ap, wkeys, 512, bank, bkey)
            yield
            S.op("act", lambda e: e.activation(out=E, in_=bank[:, 0:256], func=AF.Exp), reads=[bkey], writes=[K("E3")])
            S.op("act", lambda e: e.activation(out=vb2[:], in_=bank[:, 256:512], func=AF.Copy), reads=[bkey], writes=[K("qt")])
            S.op("act", lambda e: e.activation(out=T, in_=E, func=AF.Ln, bias=1.0), reads=[K("E3")], writes=[K("T3")])
            S.op("act", lambda e: e.activation(out=T, in_=T, func=AF.Exp, scale=-1.0), reads=[K("T3")], writes=[K("T3")])
            yield
            S.op("dve", lambda e: e.tensor_tensor(out=ka, in0=T, in1=OML[:, h0 * 128:(h0 + 2) * 128], op=ALU.mult),
                 reads=[K("T3"), "OML"], writes=[K("SL")])
            yield
            S.op("act", lambda e: e.activation(out=lf, in_=ka, func=AF.Ln, scale=-1.0, bias=1.0), reads=[K("SL")], writes=[K("E3")])
            yield
            mi = 2 + gcnt["m"] % 2
            gcnt["m"] += 1
            PM, pmk = PB[mi], ("PB", mi)
            ao = _COFF["ALL1"]

            def cum(e):
                ins = e.matmul(PM[:, 0:256], lhsT=C("SUF"), rhs=lf, start=True, stop=True)
                for hh in range(2):
                    ins = e.matmul(PM[:, 256 + 2 * hh:258 + 2 * hh], lhsT=lf[:, hh * 128:(hh + 1) * 128], rhs=cst[:, ao:ao + 2],
                                   start=True, stop=True)
                return ins
            S.op("pe", cum, reads=[K("E3"), "cst"], writes=[pmk])
            yield
            S.op("act", lambda e: e.activation(out=ekp, in_=PM[:, 0:256], func=AF.Exp), reads=[pmk], writes=[K("T3")])
            S.op("act", lambda e: e.activation(out=d["ebT"][:, 0:4], in_=PM[:, 256:260], func=AF.Exp), reads=[pmk], writes=[K("ebT")])
            yield
            S.op("dve", lambda e: e.tensor_tensor(out=kp2, in0=ka, in1=ekp, op=ALU.mult), reads=[K("SL"), K("T3")], writes=[K("qkT")])
            yield
            sui = 4 + gcnt["su"] % 2
            gcnt["su"] += 1
            PSU, psk = PB[sui], ("PB", sui)

            def su(e):
                ins = None
                for hh in range(2):
                    ins = e.matmul(PSU[:, hh * 128:(hh + 1) * 128], lhsT=kp2[:, hh * 128:(hh + 1) * 128], rhs=vb2[:, hh * 128:(hh + 1) * 128],
                                   start=True, stop=True)
                return ins
            S.op("pe", su, reads=[K("qkT"), K("qt")], writes=[psk])
            yield
            for hh in range(2):
                S.op("dve", lambda e, hh=hh: e.scalar_tensor_tensor(
                    out=S32[:, h0 + hh, :], in0=S32[:, h0 + hh, :], scalar=d["ebT"][:, 2 * hh:2 * hh + 1], in1=PSU[:, hh * 128:(hh + 1) * 128],
                    op0=ALU.mult, op1=ALU.add), reads=[("S32", h0 + hh), K("ebT"), psk], writes=[("S32", h0 + hh)])

        class _Pipe:
            def __init__(self, depth):
                self.depth = depth
                self.active = []

            def step(self):
                nxt = []
                for g in self.active:
                    try:
                        next(g)
                        nxt.append(g)
                    except StopIteration:
                        pass
                self.active = nxt

            def feed(self, g):
                while len(self.active) >= self.depth:
                    self.step()
                self.active.append(g)
                self.step()

            def drain(self):
                while self.active:
                    self.step()
        PIPE = _Pipe(NSET)

        def run_pipelined(gens, depth):
            active = []
            it = iter(gens)
            exhausted = False
            while True:
                if not exhausted and len(active) < depth:
                    try:
                        active.append(next(it))
                    except StopIteration:
                        exhausted = True
                if not active:
                    break
                nxt = []
                for g in active:
                    try:
                        next(g)
                        nxt.append(g)
                    except StopIteration:
                        pass
                active = nxt

        jobs = []

        def wtile_of(blk, lt):
            return blk * TB + lt if lt < TB else NW

        def next_ws():
            ws = cnt.setdefault("w", 0) % 2
            cnt["w"] += 1
            return ws

        def norm_job(blk, ntile):
            def comp():
                if blk == 0:
                    S.op("sp", lambda e: e.dma_start(out=gvec[:], in_=norm1_d.broadcast_to([128, D])), writes=["gvec"], dma="gvec")
                for lt in range(ntile):
                    sl = lt % 2
                    if lt < TB:
                        src = xw[(blk * TB + lt) * 128:(blk * TB + lt + 1) * 128, :]
                    else:
                        src = xs
                    S.op("sp", lambda e, sl=sl, src=src: e.dma_start(out=xt[sl][:], in_=src), writes=[("xt", sl)], dma=("xt", sl))
                    rmsnorm_rstd(xt[sl][:], [("xt", sl)], hb[sl][:], nss[sl][:], nrs[sl][:], "n%d" % sl, D, junk_key=("hb", 0))
                    S.op("dve", lambda e, sl=sl: e.scalar_tensor_tensor(out=hb[sl][:], in0=xt[sl][:], scalar=nrs[sl][:], in1=gvec[:],
                                                                         op0=ALU.mult, op1=ALU.mult),
                         reads=[("xt", sl), "n%drstd" % sl, "gvec"], writes=[("hb", 0)])
                    transposes_to(hb[sl], [("hb", 0)], KC, lambda c0, n, lt=lt: HT[:, c0:c0 + n, lt * 128:(lt + 1) * 128],
                                  [("HT", lt)], "act" if lt % 2 == 0 else "dve")
            return (None, comp)

        def hgrn_job(blk, ntile, h):
            own = blk == 3
            ws = next_ws()
            cols = [AQ, AFo, AG, AI] if own else [AFo, AI]

            def load():
                for j, co in enumerate(cols):
                    S.op("pool", lambda e, j=j, co=co: e.dma_start(
                        out=wp[ws][:, :, j * 128:(j + 1) * 128],
                        in_=w_in_d[:, co + h * 128:co + (h + 1) * 128].rearrange("(c p) n -> p c n", p=128)),
                        writes=[("wp", ws, j)], dma=("wp", ws))

            def comp():
                wkeys = [("wp", ws, j) for j in range(len(cols))]
                ncols = 128 * len(cols)
                for lt in range(TB):
                    PIPE.feed(hgrn_gen(blk, lt, h, "own" if own else "pre", wp[ws], wkeys, ncols, 2 * lt))
            return (load, comp, "hgrn")

        def smp_job(h):
            ws = next_ws()
            cols = [AQ, AFo, AG, AI]

            def load():
                for j, co in enumerate(cols):
                    S.op("pool", lambda e, j=j, co=co: e.dma_start(
                        out=wp[ws][:, :, j * 128:(j + 1) * 128],
                        in_=w_in_d[:, co + h * 128:co + (h + 1) * 128].rearrange("(c p) n -> p c n", p=128)),
                        writes=[("wp", ws, j)], dma=("wp", ws))

            def comp():
                wkeys = [("wp", ws, j) for j in range(4)]
                lt = TB
                bank, bkey = gbank()
                gemm_tile(lt, HT, [("HT", lt)], KC, wp[ws], wkeys, 512, bank, bkey)
                hgrn_item(3, lt, h, "smp", bank, bkey, lt * 128)
                if h == 7:
                    S.op("sp", lambda e: e.dma_start(out=So_d.rearrange("h d e -> d h e"), in_=S32[:]),
                         reads=[("S32", hh) for hh in range(8)], dma="SoD")
            return (load, comp)

        def hgrn_pre2_job(blk, h0):
            ws = next_ws()

            def load():
                for j, co in enumerate([AFo, AI]):
                    S.op("pool", lambda e, j=j, co=co: e.dma_start(
                        out=wp[ws][:, :, j * 256:(j + 1) * 256],
                        in_=w_in_d[:, co + h0 * 128:co + (h0 + 2) * 128].rearrange("(c p) n -> p c n", p=128)),
                        writes=[("wp", ws, 2 * j), ("wp", ws, 2 * j + 1)], dma=("wp", ws))

            def comp():
                wkeys = [("wp", ws, j) for j in range(4)]
                for lt in range(TB):
                    PIPE.feed(hgrn_pre2_gen(lt, h0, wp[ws], wkeys))
            return (load, comp, "hgrn")

        def kvq_job(blk, ntile, kind, base, p):
            own = blk == 3
            ws = next_ws()

            def load():
                S.op("pool", lambda e: e.dma_start(
                    out=wp[ws][:], in_=w_in_d[:, base + p * 512:base + (p + 1) * 512].rearrange("(c p) n -> p c n", p=128)),
                    writes=[("wp", ws, j) for j in range(4)], dma=("wp", ws))

            def comp():
                wkeys = [("wp", ws, j) for j in range(4)]
                pendk = []
                for lt in range(ntile):
                    bank, bkey = gbank()
                    gemm_tile(lt, HT, [("HT", lt)], KC, wp[ws], wkeys, 512, bank, bkey)
                    while pendk:
                        pendk.pop(0)()
                    sl = cnt.setdefault("kv", 0) % 2
                    cnt["kv"] += 1
                    wt = wtile_of(blk, lt)
                    if own and kind in ("k", "v"):
                        dst = ko_d if kind == "k" else vo_d
                        S.op("act", lambda e, sl=sl, bank=bank: e.activation(out=kst[sl][:], in_=bank[:], func=AF.Copy),
                             reads=[bkey], writes=[("kst", sl)])
                        S.op("sp", lambda e, sl=sl, dst=dst, lt=lt: e.dma_start(
                            out=dst[lt * 128:(lt + 1) * 128, p * 512:(p + 1) * 512], in_=kst[sl][:]),
                            reads=[("kst", sl)], dma=("kstD", sl))
                    S.op("dve", lambda e, sl=sl, bank=bank: e.tensor_copy(out=kbf[sl][:], in_=bank[:]), reads=[bkey], writes=[("kbf", sl)])
                    if kind == "v":
                        S.op("sp", lambda e, sl=sl, wt=wt: e.dma_start(
                            out=Vs[wt * 128:(wt + 1) * 128, p * 512:(p + 1) * 512], in_=kbf[sl][:]),
                            reads=[("kbf", sl)], writes=[("Vs", wt, p)], dma=("kbfD", sl))
                    elif kind == "k":
                        def trk_(sl=sl, wt=wt):
                            transposes_to(kbf[sl], [("kbf", sl)], 4, lambda c0, n: KTst[sl][:, c0:c0 + n, :], [("KTst", sl)], "act")
                            S.op("sp", lambda e: e.dma_start(
                                out=KTs[4 * p:4 * p + 4, :, wt * 128:(wt + 1) * 128].rearrange("h d t -> d h t"), in_=KTst[sl][:]),
                                reads=[("KTst", sl)], writes=[("KTs", wt, p)], dma=("KTstD", sl))
                        pendk.append(trk_)
                    else:
                        def trq_(sl=sl, lt=lt):
                            transposes_to(kbf[sl], [("kbf", sl)], 4,
                                          lambda c0, n: QT[:, 4 * p + c0:4 * p + c0 + n, lt * 128:(lt + 1) * 128],
                                          [("QT", lt, p)], "act")
                        pendk.append(trq_)
                while pendk:
                    pendk.pop(0)()
            return (load, comp)

        def fl_job(blk, ntile):
            own = blk == 3

            def comp():
                for lt in range(ntile):
                    bank, bkey = gbank()
                    gemm_tile(lt, HT, [("HT", lt)], KC, wfl, ["wfl"], 8, bank, bkey)
                    sl = lt % 2
                    wt = wtile_of(blk, lt)
                    S.op("dve", lambda e, sl=sl, bank=bank: e.tensor_tensor(out=fz[sl][:], in0=bank[:, 0:8], in1=BFF[:], op=ALU.add),
                         reads=[bkey, "BFF"], writes=[("fz", sl)])
                    S.op("act", lambda e, sl=sl: e.activation(out=fz[sl][:], in_=fz[sl][:], func=AF.Exp, scale=-1.0), reads=[("fz", sl)], writes=[("fz", sl)])
                    S.op("act", lambda e, sl=sl: e.activation(out=fz[sl][:], in_=fz[sl][:], func=AF.Ln, bias=1.0), reads=[("fz", sl)], writes=[("fz", sl)])
                    S.op("dve", lambda e, sl=sl, wt=wt: e.tensor_scalar(out=LFB[:, wt, :], in0=fz[sl][:], scalar1=-1.0, scalar2=None, op0=ALU.mult),
                         reads=[("fz", sl)], writes=[("LFB", wt)])
                    if own:
                        S.op("sp", lambda e, wt=wt, lt=lt: e.dma_start(out=lfo_d[lt * 128:(lt + 1) * 128, :], in_=LFB[:, wt, :]),
                             reads=[("LFB", wt)], dma="lfoD")
            return (None, comp)

        def gate_job(ntile, gp):
            ws = next_ws()

            def load():
                S.op("pool", lambda e: e.dma_start(
                    out=wp[ws][:, :, 0:PW], in_=w_in_d[:, GA + gp * PW:GA + (gp + 1) * PW].rearrange("(c p) n -> p c n", p=128)),
                    writes=[("wp", ws, j) for j in range(4)], dma=("wp", ws))

            def comp():
                wkeys = [("wp", ws, j) for j in range(4)]
                for lt in range(ntile):
                    bank, bkey = gbank()
                    gemm_tile(lt, HT, [("HT", lt)], KC, wp[ws], wkeys, PW, bank, bkey)
                    sl = cnt["kv"] % 2
                    cnt["kv"] += 1
                    S.op("act", lambda e, sl=sl, bank=bank: e.activation(out=gE[sl][:, 0:PW], in_=bank[:, 0:PW], func=AF.Exp, scale=-1.0),
                         reads=[bkey], writes=[("kst", sl)])
                    S.op("act", lambda e, sl=sl: e.activation(out=gE[sl][:, 0:PW], in_=gE[sl][:, 0:PW], func=AF.Ln, bias=1.0),
                         reads=[("kst", sl)], writes=[("kst", sl)])
                    S.op("act", lambda e, sl=sl: e.activation(out=gE[sl][:, 0:PW], in_=gE[sl][:, 0:PW], func=AF.Exp, scale=-1.0),
                         reads=[("kst", sl)], writes=[("kst", sl)])
                    S.op("sp", lambda e, sl=sl, lt=lt: e.dma_start(out=Gs[lt * 128:(lt + 1) * 128, gp * PW:(gp + 1) * PW], in_=gE[sl][:, 0:PW]),
                         reads=[("kst", sl)], writes=[("Gs", lt, gp)], dma=("kstD", sl))
            return (load, comp)

        for blk in range(4):
            own = blk == 3
            ntile = NTO if own else TB
            jobs.append(norm_job(blk, ntile))
            if own:
                for h in range(8):
                    jobs.append(hgrn_job(blk, ntile, h))
                for h in range(8):
                    jobs.append(smp_job(h))
            else:
                for h0 in range(0, 8, 2):
                    jobs.append(hgrn_pre2_job(blk, h0))
            for kind, base in (("k", BK), ("v", BV)) + ((("q", BQ),) if own else ()):
                for p in range(2):
                    jobs.append(kvq_job(blk, ntile, kind, base, p))
            jobs.append(fl_job(blk, ntile))
            if own:
                for gp in range(2 * NPW):
                    jobs.append(gate_job(ntile, gp))
        if jobs[0][0] is not None:
            jobs[0][0]()
        for k, job in enumerate(jobs):
            if len(job) < 3:
                PIPE.drain()
            if k + 1 < len(jobs) and jobs[k + 1][0] is not None:
                jobs[k + 1][0]()
            job[1]()
        PIPE.drain()
        p1.close()
        S.fence(lambda e: e.memset(CAR[:, 0, :], 0.0))

        mid2 = contextlib.ExitStack()
        obT = SB(mid2, "obT", [128, 8, TOK], BF16)
        p2 = contextlib.ExitStack()
        KTh = [SB(p2, "KTh%d" % i, [128, NW * 128], BF16) for i in range(2)]
        Vh = [SB(p2, "Vh%d" % i, [128, NW, 129], BF16) for i in range(2)]
        biasb = [SB(p2, "biasb%d" % i, [128, NW], F32) for i in range(2)]
        NPT = 8
        pT = [SB(p2, "pT%d" % i, [128, 256], BF16) for i in range(NPT)]
        rden = [SB(p2, "rden%d" % i, [128, 1], F32) for i in range(2)]
        obb = [SB(p2, "obb%d" % i, [128, 128], BF16) for i in range(2)]
        for i in range(2):
            S.op("pool", lambda e, i=i: e.memset(Vh[i][:, :, 128:129], 1.0), writes=[("Vh1", i)])

        PBC = PB[2]
        for m in range(NW + 1):
            if m > 0:
                S.op("pe", lambda e, m=m: e.matmul(PBC[:, 0:8], lhsT=C("SEL127"), rhs=CKM[:, m - 1, :], start=True, stop=True),
                     reads=[("CK", m - 1), "cst"], writes=[("PB", 2)])
                S.op("dve", lambda e, m=m: e.tensor_copy(out=CAR[:, m, :], in_=PBC[:, 0:8]), reads=[("PB", 2)], writes=[("CAR", m)])
            if m < NW:
                S.op("pe", lambda e, m=m: e.matmul(PB[3][:, 16:24], lhsT=C("UF"), rhs=LFB[:, m, :], start=True, stop=True),
                     reads=[("LFB", m), "cst"], writes=[("PB", 3)])
                S.op("dve", lambda e, m=m: e.tensor_tensor(out=CKM[:, m, :], in0=PB[3][:, 16:24], in1=CAR[:, m, :], op=ALU.add),
                     reads=[("PB", 3), ("CAR", m)], writes=[("CK", m)])
        for m in range(NW):
            S.op("dve", lambda e, m=m: e.tensor_scalar(out=CKM[:, m, :], in0=CKM[:, m, :], scalar1=KMK[:, m:m + 1], scalar2=None, op0=ALU.add),
                 reads=[("CK", m), ("CK", min(m + 1, NW - 1)), ("CAR", min(m + 1, NW)), "KMK"], writes=[("CK", m)])

        PBS = [PB[0], PB[1]]
        PBOA = [PB[4], PB[5]]
        acnt = {"s": 0, "p": 0, "o": 0}
        pend = []
        assert TB % 2 == 0

        def flush_pend():
            while pend:
                pend.pop(0)()
        for h in range(8):
            hs = h % 2
            S.op("sp", lambda e, hs=hs, h=h: e.dma_start(out=KTh[hs][:], in_=KTs[h, :, 0:NW * 128]),
                 reads=[("KTs", wt, h // 4) for wt in range(NW)], writes=[("KTh", hs)], dma=("KTh", hs))
            S.op("sp", lambda e, hs=hs, h=h: e.dma_start(out=Vh[hs][:, :, 0:128],
                                                          in_=Vs[0:NW * 128, h * 128:(h + 1) * 128].rearrange("(m p) d -> p m d", p=128)),
                 reads=[("Vs", wt, h // 4) for wt in range(NW)], writes=[("Vh", hs)], dma=("Vh", hs))
            for ip in range(TB // 2):
                i0, i1 = 2 * ip, 2 * ip + 1
                qi0, qi1 = 3 * TB + i0, 3 * TB + i1
                bs = acnt["o"] % 2
                acnt["o"] += 1
                S.op("dve", lambda e, bs=bs, h=h, qi1=qi1: e.tensor_scalar(out=biasb[bs][:], in0=CKM[:, :, h], scalar1=-1.0,
                                                                             scalar2=CAR[:, qi1 + 1, h:h + 1], op0=ALU.mult, op1=ALU.add),
                     reads=[("CK", m) for m in range(NW)] + [("CAR", qi1 + 1)], writes=[("biasb", bs)])
                nkt = qi1 + 1
                m0 = 0
                while m0 < nkt:
                    ng = min(2, nkt - m0)
                    sbi = acnt["s"] % 4
                    acnt["s"] += 1
                    sb_ = PB[sbi]
                    skey = ("PB", sbi)

                    def scm(e, sb_=sb_, hs=hs, m0=m0, ng=ng, h=h, i0=i0, i1=i1, qi1=qi1):
                        ins = None
                        for k in range(ng):
                            m = m0 + k
                            if m == qi1:
                                ins = e.matmul(sb_[:, k * 256 + 128:(k + 1) * 256], lhsT=KTh[hs][:, m * 128:(m + 1) * 128],
                                               rhs=QT[:, h, i1 * 128:(i1 + 1) * 128], start=True, stop=True)
                            else:
                                ins = e.matmul(sb_[:, k * 256:(k + 1) * 256], lhsT=KTh[hs][:, m * 128:(m + 1) * 128],
                                               rhs=QT[:, h, i0 * 128:(i1 + 1) * 128], start=True, stop=True)
                        return ins
                    S.op("pe", scm, reads=[("KTh", hs), ("QT", i0, h // 4), ("QT", i1, h // 4)], writes=[skey])
                    flush_pend()
                    pss = []
                    for k in range(ng):
                        m = m0 + k
                        ps_ = acnt["p"] % NPT
                        acnt["p"] += 1
                        pss.append(ps_)
                        c0 = 128 if m == qi1 else 0
                        S.op("act", lambda e, ps_=ps_, sb_=sb_, k=k, bs=bs, m=m, c0=c0: e.activation(
                            out=pT[ps_][:, c0:256], in_=sb_[:, k * 256 + c0:(k + 1) * 256], func=AF.Exp, scale=SC, bias=biasb[bs][:, m:m + 1]),
                            reads=[skey, ("biasb", bs)], writes=[("pT", ps_)])
                        if m == qi0:
                            S.op("pool", lambda e, ps_=ps_: e.tensor_tensor(out=pT[ps_][:, 0:128], in0=pT[ps_][:, 0:128], in1=C("UF"), op=ALU.mult),
                                 reads=[("pT", ps_), "cst"], writes=[("pT", ps_)])
                        if m == qi1:
                            S.op("pool", lambda e, ps_=ps_: e.tensor_tensor(out=pT[ps_][:, 128:256], in0=pT[ps_][:, 128:256], in1=C("UF"), op=ALU.mult),
                                 reads=[("pT", ps_), "cst"], writes=[("pT", ps_)])

                    def pvs(pss=pss, m0=m0, ng=ng, hs=hs, qi0=qi0, qi1=qi1):
                        def pvm(e):
                            ins = None
                            for k in range(ng):
                                m = m0 + k
                                if m <= qi0:
                                    ins = e.matmul(PB[4][:, 0:129], lhsT=pT[pss[k]][:, 0:128], rhs=Vh[hs][:, m, :], start=(m == 0), stop=(m == qi0))
                                ins = e.matmul(PB[5][:, 0:129], lhsT=pT[pss[k]][:, 128:256], rhs=Vh[hs][:, m, :], start=(m == 0), stop=(m == qi1))
                            return ins
                        S.op("pe", pvm, reads=[("pT", p_) for p_ in pss] + [("Vh", hs), ("Vh1", hs)], writes=[("PB", 4), ("PB", 5)])
                    pend.append(pvs)
                    m0 += ng

                def epi(h=h, i0=i0):
                    for j in range(2):
                        i = i0 + j
                        S.op("dve", lambda e, j=j: e.reciprocal(out=rden[j][:], in_=PB[4 + j][:, 128:129]),
                             reads=[("PB", 4 + j)], writes=[("rden", j)])
                        S.op("dve", lambda e, j=j: e.tensor_scalar(out=obb[j][:], in0=PB[4 + j][:, 0:128], scalar1=rden[j][:],
                                                                   scalar2=None, op0=ALU.mult),
                             reads=[("PB", 4 + j), ("rden", j)], writes=[("obb", j)])
                        transposes_to(obb[j], [("obb", j)], 1, lambda c0, n, i=i: obT[:, h:h + 1, i * 128:(i + 1) * 128], [("obT", h, i)], "act")
                pend.append(epi)
        flush_pend()

        p2.close()
        S.fence(lambda e: e.memset(CAR[:, 0, :], 0.0))
        p2b = contextlib.ExitStack()
        rdenb = [SB(p2b, "rdenb%d" % i, [128, 1], F32) for i in range(2)]
        CLF = SB(p2b, "CLF", [128, PT, 4, 8], F32)
        CKc = SB(p2b, "CKc", [128, PT, 4, 8], F32)
        CARc = SB(p2b, "CARc", [128, 32], F32)
        Rb = SB(p2b, "Rb", [128, 32], F32)
        Rn = SB(p2b, "Rn", [128, 32], F32)
        BN = SB(p2b, "BN", [128, 8], F32)
        Kc = [SB(p2b, "Kc%d" % i, [128, PT, 128], BF16) for i in range(2)]
        KcT = [SB(p2b, "KcT%d" % i, [128, PT * 128], BF16) for i in range(2)]
        Vc = [SB(p2b, "Vc%d" % i, [128, PT, 129], BF16) for i in range(2)]
        biasc = [SB(p2b, "biasc%d" % i, [128, PT], F32) for i in range(2)]
        Ec = [SB(p2b, "Ec%d" % i, [128, PT], F32) for i in range(2)]
        exs = [SB(p2b, "exs%d" % i, [128, 16, 32], F32) for i in range(2)]
        pTs = [SB(p2b, "pTs%d" % i, [128, PT, 32], BF16) for i in range(2)]
        KTn = SB(p2b, "KTn", [128, 8, 128], BF16)
        Vn = SB(p2b, "Vn", [128, 8, 129], BF16)
        pnf = SB(p2b, "pnf", [128, 128], F32)
        pnT = [SB(p2b, "pnT%d" % i, [128, 128], BF16) for i in range(2)]
        obs = [SB(p2b, "obs%d" % i, [32, 128], BF16) for i in range(2)]

        for i in range(2):
            S.op("pool", lambda e, i=i: e.memset(Vc[i][:, :, 128:129], 1.0), writes=[("Vc1", i)])
        S.op("pool", lambda e: e.memset(Vn[:, :, 128:129], 1.0), writes=["Vn1"])
        for r in range(4):
            S.op("sp", lambda e, r=r: e.dma_start(out=CLF[:, :, r, :], in_=clf_d[r].rearrange("(m p) h -> p m h", p=128)),
                 writes=[("CLF", r)], dma="CLF")
        clfk = [("CLF", r) for r in range(4)]
        for m in range(PT):
            if m > 0:
                S.op("pe", lambda e, m=m: e.matmul(PBC[:, 32:64], lhsT=C("SEL127"), rhs=CKc[:, m - 1, :, :].rearrange("p r h -> p (r h)"),
                                                   start=True, stop=True), reads=[("CKc", m - 1), "cst"], writes=[("PB", 2)])
                S.op("dve", lambda e: e.tensor_copy(out=CARc[:], in_=PBC[:, 32:64]), reads=[("PB", 2)], writes=["CARc"])
            S.op("pe", lambda e, m=m: e.matmul(PB[3][:, 64:96], lhsT=C("UF"), rhs=CLF[:, m, :, :].rearrange("p r h -> p (r h)"),
                                               start=True, stop=True), reads=clfk + ["cst"], writes=[("PB", 3)])
            if m > 0:
                S.op("dve", lambda e, m=m: e.tensor_tensor(out=CKc[:, m, :, :].rearrange("p r h -> p (r h)"), in0=PB[3][:, 64:96], in1=CARc[:], op=ALU.add),
                     reads=[("PB", 3), "CARc"], writes=[("CKc", m)])
            else:
                S.op("dve", lambda e, m=m: e.tensor_copy(out=CKc[:, m, :, :].rearrange("p r h -> p (r h)"), in_=PB[3][:, 64:96]),
                     reads=[("PB", 3)], writes=[("CKc", m)])
        def tot(e):
            ins = None
            for m in range(PT):
                ins = e.matmul(PBC[:, 96:128], lhsT=C("ALL1"), rhs=CLF[:, m, :, :].rearrange("p r h -> p (r h)"), start=(m == 0), stop=(m == PT - 1))
            return ins
        S.op("pe", tot, reads=clfk + ["cst"], writes=[("PB", 2)])
        S.op("dve", lambda e: e.tensor_copy(out=Rb[:], in_=PBC[:, 96:128]), reads=[("PB", 2)], writes=["Rb"])
        def newsum(e):
            ins = None
            for r in range(4):
                ins = e.matmul(PBC[:, 128 + 8 * r:136 + 8 * r], lhsT=C("SR4_%d" % r), rhs=LFB[:, NW, :], start=True, stop=True)
            return ins
        S.op("pe", newsum, reads=[("LFB", NW), "cst"], writes=[("PB", 2)])
        S.op("dve", lambda e: e.tensor_tensor(out=Rb[:], in0=PBC[:, 128:160], in1=Rb[:], op=ALU.add), reads=[("PB", 2), "Rb"], writes=["Rb"])
        S.op("pe", lambda e: e.matmul(PBC[:, 160:168], lhsT=C("SU4"), rhs=LFB[:, NW, :], start=True, stop=True),
             reads=[("LFB", NW), "cst"], writes=[("PB", 2)])
        S.op("dve", lambda e: e.tensor_copy(out=BN[:], in_=PBC[:, 160:168]), reads=[("PB", 2)], writes=["BN"])

        S.op("sp", lambda e: e.dma_start(out=KTn[:], in_=KTs[:, :, NW * 128:(NW + 1) * 128].rearrange("h d t -> d h t")),
             reads=[("KTs", NW, 0), ("KTs", NW, 1)], writes=["KTn"], dma="KTn")
        S.op("sp", lambda e: e.dma_start(out=Vn[:, :, 0:128], in_=Vs[NW * 128:(NW + 1) * 128, :].rearrange("p (h d) -> p h d", h=8)),
             reads=[("Vs", NW, 0), ("Vs", NW, 1)], writes=["Vn"], dma="Vn")

        PBSS = [PB[0], PB[1]]
        PBOS = [PB[4], PB[5]]
        PBN = PB[3]
        scnt = {"k": 0, "s": 0, "o": 0}
        s0 = TB * 128
        for h in range(8):
            hn = h % 2
            S.op("pe", lambda e, h=h: e.matmul(PBN[:, 0:128], lhsT=KTn[:, h, :], rhs=QT[:, h, s0:s0 + 128], start=True, stop=True),
                 reads=["KTn", ("QT", TB, h // 4)], writes=[("PB", 3)])
            S.op("act", lambda e, h=h: e.activation(out=pnf[:], in_=PBN[:, 0:128], func=AF.Exp, scale=SC, bias=BN[:, h:h + 1]),
                 reads=[("PB", 3), "BN"], writes=["pnf"])
            S.op("dve", lambda e, hn=hn: e.tensor_tensor(out=pnT[hn][:], in0=pnf[:], in1=C("U4"), op=ALU.mult),
                 reads=["pnf", "cst"], writes=[("pnT", hn)])
            for r in range(4):
                ks = scnt["k"] % 2
                scnt["k"] += 1
                S.op("pool", lambda e, ks=ks, r=r, h=h: e.dma_start(out=Kc[ks][:], in_=ck_d[r, :, h * 128:(h + 1) * 128].rearrange("(m p) d -> p m d", p=128)),
                     writes=[("Kc", ks)], dma=("Kc", ks))
                S.op("pool", lambda e, ks=ks, r=r, h=h: e.dma_start(out=Vc[ks][:, :, 0:128], in_=cv_d[r, :, h * 128:(h + 1) * 128].rearrange("(m p) d -> p m d", p=128)),
                     writes=[("Vc", ks)], dma=("Vc", ks))
                m0 = 0
                while m0 < PT:
                    n = min(8, PT - m0)
                    tb, tk = tbank()

                    def trk(e, tb=tb, ks=ks, m0=m0, n=n):
                        ins = None
                        for c in range(n):
                            ins = e.transpose(out=tb[:, c * 128:(c + 1) * 128], in_=Kc[ks][:, m0 + c, :], identity=idb[:])
                        return ins
                    S.op("pe", trk, reads=[("Kc", ks), "idb"], writes=[tk])
                    if (m0 // 8) % 2 == 0:
                        S.op("act", lambda e, tb=tb, ks=ks, m0=m0, n=n: e.activation(out=KcT[ks][:, m0 * 128:(m0 + n) * 128], in_=tb[:, 0:n * 128], func=AF.Copy),
                             reads=[tk], writes=[("KcT", ks, m0)])
                    else:
                        S.op("dve", lambda e, tb=tb, ks=ks, m0=m0, n=n: e.tensor_copy(out=KcT[ks][:, m0 * 128:(m0 + n) * 128], in_=tb[:, 0:n * 128]),
                             reads=[tk], writes=[("KcT", ks, m0)])
                    m0 += n
                S.op("dve", lambda e, ks=ks, r=r, h=h: e.tensor_scalar(out=biasc[ks][:], in0=CKc[:, :, r, h], scalar1=-1.0,
                                                                         scalar2=Rb[:, r * 8 + h:r * 8 + h + 1], op0=ALU.mult, op1=ALU.add),
                     reads=[("CKc", m) for m in range(PT)] + ["Rb"], writes=[("biasc", ks)])
                S.op("act", lambda e, ks=ks: e.activation(out=Ec[ks][:], in_=biasc[ks][:], func=AF.Exp), reads=[("biasc", ks)], writes=[("Ec", ks)])
                q0 = s0 + 32 * r
                m0 = 0
                while m0 < PT:
                    n = min(16, PT - m0)
                    sb_i = scnt["s"] % 2
                    scnt["s"] += 1
                    sbk = PBSS[sb_i]

                    def scm(e, sbk=sbk, ks=ks, m0=m0, n=n, h=h, q0=q0):
                        ins = None
                        for c in range(n):
                            ins = e.matmul(sbk[:, c * 32:(c + 1) * 32], lhsT=KcT[ks][:, (m0 + c) * 128:(m0 + c + 1) * 128],
                                           rhs=QT[:, h, q0:q0 + 32], start=True, stop=True)
                        return ins
                    S.op("pe", scm, reads=[("KcT", ks, (m0 // 8) * 8), ("KcT", ks, (m0 // 8) * 8 + 8), ("QT", TB, h // 4)], writes=[("PB", sb_i)])
                    S.op("act", lambda e, sbk=sbk, sb_i=sb_i, n=n: e.activation(out=exs[sb_i][:, 0:n, :].rearrange("p m t -> p (m t)"),
                                                                                in_=sbk[:, 0:n * 32], func=AF.Exp, scale=SC),
                         reads=[("PB", sb_i)], writes=[("exs", sb_i)])
                    S.op("dve", lambda e, sb_i=sb_i, ks=ks, m0=m0, n=n: e.tensor_tensor(
                        out=pTs[ks][:, m0:m0 + n, :], in0=exs[sb_i][:, 0:n, :],
                        in1=Ec[ks][:, m0:m0 + n].unsqueeze(2).to_broadcast([128, n, 32]), op=ALU.mult),
                        reads=[("exs", sb_i), ("Ec", ks)], writes=[("pTs", ks, m0)])
                    m0 += n
                ob_i = scnt["o"] % 2
                scnt["o"] += 1
                obk = PBOS[ob_i]

                def pv(e, obk=obk, ks=ks, hn=hn, r=r, h=h):
                    ins = None
                    for m in range(PT):
                        ins = e.matmul(obk[0:32, 0:129], lhsT=pTs[ks][:, m, :], rhs=Vc[ks][:, m, :], start=(m == 0), stop=False)
                    ins = e.matmul(obk[0:32, 0:129], lhsT=pnT[hn][:, 32 * r:32 * r + 32], rhs=Vn[:, h, :], start=False, stop=True)
                    return ins
                S.op("pe", pv, reads=[("pTs", ks, m0) for m0 in range(0, PT, 16)] + [("Vc", ks), ("Vc1", ks), ("pnT", hn), "Vn", "Vn1"],
                     writes=[("PB", 4 + ob_i)])
                S.op("dve", lambda e, ob_i=ob_i, obk=obk: e.reciprocal(out=rdenb[ob_i][0:32, :], in_=obk[0:32, 128:129]),
                     reads=[("PB", 4 + ob_i)], writes=[("rdenb", ob_i)])
                S.op("dve", lambda e, ob_i=ob_i, obk=obk: e.tensor_scalar(out=obs[ob_i][:], in0=obk[0:32, 0:128], scalar1=rdenb[ob_i][0:32, :],
                                                                           scalar2=None, op0=ALU.mult),
                     reads=[("PB", 4 + ob_i), ("rdenb", ob_i)], writes=[("obs", ob_i)])
                tb, tk = tbank()
                S.op("pe", lambda e, tb=tb, ob_i=ob_i: e.transpose(out=tb[:, 0:32], in_=obs[ob_i][:], identity=idb[0:32, 0:32]),
                     reads=[("obs", ob_i), "idb"], writes=[tk])
                S.op("act", lambda e, tb=tb, h=h, q0=q0: e.activation(out=obT[:, h, q0:q0 + 32], in_=tb[:, 0:32], func=AF.Copy),
                     reads=[tk], writes=[("obT", h, TB, r)])
        p2b.close()
        S.fence(lambda e: e.memset(CAR[:, 0, :], 0.0))

        p3 = contextlib.ExitStack()
        wpa = [SB(p3, "wpa%d" % i, [128, 8, PW], BF16) for i in range(2)]
        wpb = [SB(p3, "wpb%d" % i, [128, 8, PW], BF16) for i in range(2)]
        gab = [SB(p3, "gab%d" % i, [128, 2, PW], F32) for i in range(3)]
        t1 = [SB(p3, "t1_%d" % i, [128, PW], F32) for i in range(3)]
        t2 = [SB(p3, "t2_%d" % i, [128, PW], F32) for i in range(3)]
        mb = [SB(p3, "mb%d" % i, [128, PW], BF16) for i in range(3)]
        oakeys = lambda lt: [("oaT", h, lt) for h in range(8)]
        obkeys = lambda lt: ([("obT", h, lt) for h in range(8)] if lt < TB else [("obT", h, TB, r) for h in range(8) for r in range(4)])
        pend3 = []
        u = 0
        for n_ in range(NPW):
            ws = n_ % 2
            S.op("pool", lambda e, ws=ws, n_=n_: e.dma_start(out=wpa[ws][:], in_=w_pa_d[:, n_ * PW:(n_ + 1) * PW].rearrange("(c p) n -> p c n", p=128)),
                 writes=[("wpa", ws)], dma=("wpa", ws))
            S.op("pool", lambda e, ws=ws, n_=n_: e.dma_start(out=wpb[ws][:], in_=w_pb_d[:, n_ * PW:(n_ + 1) * PW].rearrange("(c p) n -> p c n", p=128)),
                 writes=[("wpb", ws)], dma=("wpb", ws))
            for lt in range(NTO):
                sl = u % 3
                pb2 = u % 2
                u += 1
                S.op("sp", lambda e, sl=sl, lt=lt, n_=n_: e.dma_start(
                    out=gab[sl][:], in_=Gs[lt * 128:(lt + 1) * 128, :].rearrange("p (g c) -> p g c", g=2)[:, :, n_ * PW:(n_ + 1) * PW]),
                    reads=[("Gs", lt, n_), ("Gs", lt, NPW + n_)], writes=[("gab", sl)], dma=("gab", sl))
                ba, bak = PB[pb2], ("PB", pb2)
                gemm_tile(lt, oaT, oakeys(lt), 8, wpa[ws], [("wpa", ws)], PW, ba, bak)
                bb, bbk = PB[2 + pb2], ("PB", 2 + pb2)
                gemm_tile(lt, obT, obkeys(lt), 8, wpb[ws], [("wpb", ws)], PW, bb, bbk)
                while pend3:
                    pend3.pop(0)()
                S.op("dve", lambda e, sl=sl, ba=ba: e.tensor_tensor(out=t1[sl][:], in0=ba[:, 0:PW], in1=gab[sl][:, 0, :], op=ALU.mult),
                     reads=[bak, ("gab", sl)], writes=[("t1", sl)])
                S.op("dve", lambda e, sl=sl, bb=bb: e.tensor_tensor(out=t2[sl][:], in0=bb[:, 0:PW], in1=gab[sl][:, 1, :], op=ALU.mult),
                     reads=[bbk, ("gab", sl)], writes=[("t2", sl)])
                S.op("dve", lambda e, sl=sl: e.tensor_tensor(out=mb[sl][:], in0=t1[sl][:], in1=t2[sl][:], op=ALU.add),
                     reads=[("t1", sl), ("t2", sl)], writes=[("mb", sl)])

                def trs(sl=sl, lt=lt, n_=n_):
                    transposes_to(mb[sl], [("mb", sl)], PWC, lambda c0, n: HT[:, n_ * PWC + c0:n_ * PWC + c0 + n, lt * 128:(lt + 1) * 128],
                                  [("HT", lt, "m", n_)], "act")
                pend3.append(trs)
        while pend3:
            pend3.pop(0)()
        p3.close()
        mid2.close()
        mid.close()
        S.fence(lambda e: e.memset(CAR[:, 0, :], 0.0))

        late = contextlib.ExitStack()
        X1 = SB(late, "X1", [128, NTO, D], F32)
        p4 = contextlib.ExitStack()
        wo = [SB(p4, "wo%d" % i, [128, KC, PW], BF16) for i in range(2)]
        xr = [SB(p4, "xr%d" % i, [128, PW], F32) for i in range(2)]
        mkeys = lambda lt: [("HT", lt, "m", n_) for n_ in range(NPW)]
        for n_ in range(NPW):
            ws = n_ % 2
            S.op("pool", lambda e, ws=ws, n_=n_: e.dma_start(out=wo[ws][:], in_=w_o_d[:, n_ * PW:(n_ + 1) * PW].rearrange("(c p) n -> p c n", p=128)),
                 writes=[("wo", ws)], dma=("wo", ws))
            for lt in range(NTO):
                sl = lt % 2
                src = xw[(3 * TB + lt) * 128:(3 * TB + lt + 1) * 128, n_ * PW:(n_ + 1) * PW] if lt < TB else xs[:, n_ * PW:(n_ + 1) * PW]
                S.op("sp", lambda e, sl=sl, src=src: e.dma_start(out=xr[sl][:], in_=src), writes=[("xr", sl)], dma=("xr", sl))
                bank, bkey = gbank()
                gemm_tile(lt, HT, mkeys(lt), KC, wo[ws], [("wo", ws)], PW, bank, bkey)
                S.op("dve", lambda e, sl=sl, bank=bank, lt=lt, n_=n_: e.tensor_tensor(out=X1[:, lt, n_ * PW:(n_ + 1) * PW], in0=bank[:, 0:PW],
                                                                                     in1=xr[sl][:], op=ALU.add),
                     reads=[bkey, ("xr", sl)], writes=[("X1", lt, n_)])
        p4.close()
        S.fence(lambda e: e.memset(CAR[:, 0, :], 0.0))
        p5 = contextlib.ExitStack()
        nss2 = [SB(p5, "nss2_%d" % i, [128, 1], F32) for i in range(2)]
        nrs2 = [SB(p5, "nrs2_%d" % i, [128, 1], F32) for i in range(2)]
        hb2 = [SB(p5, "hb2_%d" % i, [128, D], BF16) for i in range(2)]
        w1p = [SB(p5, "w1p%d" % i, [128, KC, FW], BF16) for i in range(2)]
        w2p = [SB(p5, "w2p%d" % i, [128, FWC, D], BF16) for i in range(2)]
        uT = [SB(p5, "uT%d" % i, [128, FWC, TOK], BF16) for i in range(2)]
        usq = [SB(p5, "usq%d" % i, [128, 512], F32) for i in range(2)]
        S.op("sp", lambda e: e.dma_start(out=gvec[:], in_=norm2_d.broadcast_to([128, D])),
             reads=[], writes=["gvec"], dma="gvec")
        x1keys = lambda lt: [("X1", lt, n_) for n_ in range(NPW)]
        for lt in range(NTO):
            sl = lt % 2
            rmsnorm_rstd(X1[:, lt, :], x1keys(lt), hb2[sl][:], nss2[sl][:], nrs2[sl][:], "m%d" % sl, D, junk_key=("hb2", sl))
            S.op("dve", lambda e, sl=sl, lt=lt: e.scalar_tensor_tensor(out=hb2[sl][:], in0=X1[:, lt, :], scalar=nrs2[sl][:], in1=gvec[:],
                                                                         op0=ALU.mult, op1=ALU.mult),
                 reads=x1keys(lt) + ["m%drstd" % sl, "gvec"], writes=[("hb2", sl)])
            transposes_to(hb2[sl], [("hb2", sl)], KC, lambda c0, n, lt=lt: HT[:, c0:c0 + n, lt * 128:(lt + 1) * 128],
                          [("HT", lt, "h2")], "act" if lt % 2 == 0 else "dve")

        tgroups = []
        t0 = 0
        while t0 < TOK:
            n = min(512, TOK - t0)
            tgroups.append((t0, n))
            t0 += n
        h2keys = [("HT", lt, "h2") for lt in range(NTO)]
        PBU = [PB[2], PB[3]]
        PBY = [PB[0], PB[1], PB[4], PB[5]]
        ucnt = {"u": 0, "y": 0}
        for fp in range(NFP):
            ws = fp % 2
            S.op("pool", lambda e, ws=ws, fp=fp: e.dma_start(out=w1p[ws][:], in_=w1_d[:, fp * FW:(fp + 1) * FW].rearrange("(c p) n -> p c n", p=128)),
                 writes=[("w1p", ws)], dma=("w1p", ws))
            S.op("pool", lambda e, ws=ws, fp=fp: e.dma_start(out=w2p[ws][:], in_=w2_d[fp * FW:(fp + 1) * FW, :].rearrange("(c p) n -> p c n", p=128)),
                 writes=[("w2p", ws)], dma=("w2p", ws))
            for j in range(FWC):
                for (t0, n) in tgroups:
                    ui = ucnt["u"] % 2
                    ucnt["u"] += 1
                    ub = PBU[ui]

                    def mm1(e, ub=ub, ws=ws, j=j, t0=t0, n=n):
                        ins = None
                        for c in range(KC):
                            ins = e.matmul(ub[:, 0:n], lhsT=w1p[ws][:, c, j * 128:(j + 1) * 128], rhs=HT[:, c, t0:t0 + n],
                                           start=(c == 0), stop=(c == KC - 1))
                        return ins
                    S.op("pe", mm1, reads=h2keys + [("w1p", ws)], writes=[("PB", 2 + ui)])
                    S.op("act", lambda e, ui=ui, ub=ub, n=n: e.activation(out=usq[ui][:, 0:n], in_=ub[:, 0:n], func=AF.Square),
                         reads=[("PB", 2 + ui)], writes=[("usq", ui)])
                    S.op("dve", lambda e, ui=ui, ub=ub, n=n, ws=ws, j=j, t0=t0: e.scalar_tensor_tensor(
                        out=uT[ws][:, j, t0:t0 + n], in0=ub[:, 0:n], scalar=0.0, in1=usq[ui][:, 0:n], op0=ALU.is_gt, op1=ALU.mult),
                        reads=[("PB", 2 + ui), ("usq", ui)], writes=[("uT", ws, j, t0)])
            ukeys = [("uT", ws, j, t0) for j in range(FWC) for (t0, n) in tgroups]
            for lt in range(NTO):
                for n_ in range(NPW):
                    yi = ucnt["y"] % 4
                    ucnt["y"] += 1
                    yb = PBY[yi]

                    def mm2(e, yb=yb, ws=ws, lt=lt, n_=n_):
                        ins = None
                        for c in range(FWC):
                            ins = e.matmul(yb[:, 0:PW], lhsT=uT[ws][:, c, lt * 128:(lt + 1) * 128], rhs=w2p[ws][:, c, n_ * PW:(n_ + 1) * PW],
                                           start=(c == 0), stop=(c == FWC - 1))
                        return ins
                    S.op("pe", mm2, reads=ukeys + [("w2p", ws)], writes=[("PB", (0, 1, 4, 5)[yi])])
                    S.op("dve", lambda e, yb=yb, lt=lt, n_=n_: e.tensor_tensor(out=X1[:, lt, n_ * PW:(n_ + 1) * PW], in0=yb[:, 0:PW],
                                                                               in1=X1[:, lt, n_ * PW:(n_ + 1) * PW], op=ALU.add),
                         reads=[("PB", (0, 1, 4, 5)[yi]), ("X1", lt, n_)], writes=[("X1", lt, n_)])

        S.op("sp", lambda e: e.dma_start(out=gvec[:], in_=normf_d.broadcast_to([128, D])), writes=["gvec"], dma="gvec")
        for lt in range(NTO):
            sl = lt % 2
            rmsnorm_rstd(X1[:, lt, :], x1keys(lt), hb2[sl][:], nss2[sl][:], nrs2[sl][:], "m%d" % sl, D, junk_key=("hb2", sl))
            S.op("dve", lambda e, sl=sl, lt=lt: e.scalar_tensor_tensor(out=X1[:, lt, :], in0=X1[:, lt, :], scalar=nrs2[sl][:], in1=gvec[:],
                                                                         op0=ALU.mult, op1=ALU.mult),
                 reads=x1keys(lt) + ["m%drstd" % sl, "gvec"], writes=x1keys(lt))
            S.op("sp", lambda e, sl=sl, lt=lt: e.dma_start(out=y_d[lt * 128:(lt + 1) * 128, :], in_=X1[:, lt, :]),
                 reads=x1keys(lt), dma=("ystD", sl))
        p5.close()
        late.close()

        finals = [s for s in ("SoD", ("SSoD", 0), ("SSoD", 1), ("kstD", 0), ("kstD", 1), "lfoD", ("ystD", 0), ("ystD", 1))]
        S.emit(outer, final_dma_slots=finals)
    return nc, S


_CACHE = {}


def _run(cfg, x_prompt, x_sample, cache_fox_k, cache_fox_v, cache_fox_logf, state_hgrn,
         norm1, w_in, b_fox_f, lb_logits, gnorm_a, w_pa, w_pb, w_o, norm2, w1, w2, norm_f):
    D, TB, PT = cfg["D"], cfg["TB"], cfg["PT"]
    NW = 4 * TB
    BLK = TB * 128
    NTO = TB + 1
    f = lambda a: np.ascontiguousarray(np.asarray(a, dtype=np.float32))
    x_prompt, x_sample = f(x_prompt), f(x_sample)
    ck, cv, clf, st0 = f(cache_fox_k)[0], f(cache_fox_v)[0], f(cache_fox_logf)[0], f(state_hgrn)[0]
    key = (D, TB, PT)
    if key not in _CACHE:
        _CACHE[key] = build(cfg)[0]
    nc = _CACHE[key]
    shared = {
        "norm1": f(norm1).reshape(1, D), "w_in": f(w_in)[0], "b_fox_f": f(b_fox_f).reshape(1, 8),
        "lb_logits": f(lb_logits), "gnorm_a": f(gnorm_a).reshape(1, 128), "w_pa": f(w_pa)[0], "w_pb": f(w_pb)[0],
        "w_o": f(w_o)[0], "norm2": f(norm2).reshape(1, D), "w1": f(w1)[0], "w2": f(w2)[0],
        "norm_f": f(norm_f).reshape(1, D), "consts": _CST,
    }
    in_maps = []
    for c in range(8):
        b, j = c // 4, c % 4
        xwin = np.zeros((NW * 128, D), np.float32)
        kmask = np.zeros((128, NW), np.float32)
        for p in range(4):
            src_blk = j - 3 + p
            if src_blk >= 0:
                xwin[p * BLK:(p + 1) * BLK] = x_prompt[b, src_blk * BLK:(src_blk + 1) * BLK]
            else:
                kmask[:, p * TB:(p + 1) * TB] = BIG
        m = dict(shared)
        m["xw"] = xwin
        m["xs"] = np.ascontiguousarray(x_sample[4 * c:4 * c + 4].reshape(128, D))
        m["kmask"] = kmask
        m["ck"] = np.ascontiguousarray(ck[4 * c:4 * c + 4].reshape(4, PT * 128, 1024))
        m["cv"] = np.ascontiguousarray(cv[4 * c:4 * c + 4].reshape(4, PT * 128, 1024))
        m["clf"] = np.ascontiguousarray(clf[4 * c:4 * c + 4])
        m["st0"] = np.ascontiguousarray(st0[4 * c:4 * c + 4])
        in_maps.append(m)
    res = run_bass_kernel_spmd(nc, in_maps, core_ids=list(range(8))).results
    SEQ = 4 * BLK
    y_p = np.zeros((2, SEQ, D), np.float32)
    y_s = np.zeros((32, 32, D), np.float32)
    k_p = np.zeros((1, 2, SEQ, 8, 128), np.float32)
    v_p = np.zeros((1, 2, SEQ, 8, 128), np.float32)
    lf_p = np.zeros((1, 2, SEQ, 8), np.float32)
    S_p = np.zeros((1, 2, 8, 128, 128), np.float32)
    k_s = np.zeros((1, 32, 32, 8, 128), np.float32)
    v_s = np.zeros((1, 32, 32, 8, 128), np.float32)
    lf_s = np.zeros((1, 32, 32, 8), np.float32)
    S_s = np.zeros((1, 32, 8, 128, 128), np.float32)
    for c in range(8):
        b, j = c // 4, c % 4
        r = res[c]
        sl = slice(j * BLK, (j + 1) * BLK)
        y_p[b, sl] = r["y"][:BLK]
        y_s[4 * c:4 * c + 4] = r["y"][BLK:].reshape(4, 32, D)
        k_p[0, b, sl] = r["ko"][:BLK].reshape(BLK, 8, 128)
        v_p[0, b, sl] = r["vo"][:BLK].reshape(BLK, 8, 128)
        lf_p[0, b, sl] = r["lfo"][:BLK]
        k_s[0, 4 * c:4 * c + 4] = r["ko"][BLK:].reshape(4, 32, 8, 128)
        v_s[0, 4 * c:4 * c + 4] = r["vo"][BLK:].reshape(4, 32, 8, 128)
        lf_s[0, 4 * c:4 * c + 4] = r["lfo"][BLK:].reshape(4, 32, 8)
        S_s[0, 4 * c:4 * c + 4] = r["Ss"]
        if j == 3:
            S_p[0, b] = r["So"]
    return (y_p, y_s, k_p, v_p, lf_p, S_p, k_s, v_s, lf_s, S_s)


def kernel(**inputs):
    cfg = {"D": 2048, "TB": 8, "PT": 32}
    return _run(cfg, **inputs)
```
